# Optimizing a Trainium2 kernel written in Bass

```python
import jax, jax.numpy as jnp
from jax import lax
import numpy as np

D_MODEL = 1024
BATCH = 1
SEQ = 16384
DEPTH = 1
DEC_BATCH = 32
DEC_SEQ = 16
PAST_LEN = 2048

CHUNK = 64
Q_BLOCK = 128
MIX_WIDTH = D_MODEL
CONV_DIM = MIX_WIDTH // 2
ATTN_DIM = MIX_WIDTH - CONV_DIM
QK_DIM = 64
N_HEADS = ATTN_DIM // (2 * QK_DIM)
V_DIM = 2 * QK_DIM
CONV_W = 3
D_FF = 4 * D_MODEL
LN_EPS = 1e-5
ALPHA = (2 * DEPTH) ** 0.25
BETA = (8 * DEPTH) ** -0.25
QK_COLS = N_HEADS * 2 * QK_DIM
V_COLS = N_HEADS * V_DIM
IN_WIDTH = 3 * CONV_DIM + 2 * QK_COLS + V_COLS

kernel_name = "hybrid_conv_diffattn_stream_step"


def layer_norm(x, g, b):
    xf = x.astype(jnp.float32)
    mu = xf.mean(-1, keepdims=True)
    var = jnp.square(xf - mu).mean(-1, keepdims=True)
    return ((xf - mu) * lax.rsqrt(var + LN_EPS) * g.astype(jnp.float32) + b.astype(jnp.float32)).astype(x.dtype)


def rms_norm(x, g):
    xf = x.astype(jnp.float32)
    ms = jnp.square(xf).mean(-1, keepdims=True)
    return (xf * lax.rsqrt(ms + LN_EPS) * g.astype(jnp.float32)).astype(x.dtype)


def project(x, w_in):
    B, S, _ = x.shape
    z = jnp.einsum('bsd,de->bse', x, w_in)
    o1, o2, o3 = CONV_DIM, 2 * CONV_DIM, 3 * CONV_DIM
    o4, o5 = o3 + QK_COLS, o3 + 2 * QK_COLS
    gate_b = z[..., :o1]
    u = z[..., o1:o2] * z[..., o2:o3]
    q = z[..., o3:o4].reshape(B, S, N_HEADS, 2, QK_DIM)
    k = z[..., o4:o5].reshape(B, S, N_HEADS, 2, QK_DIM)
    v = z[..., o5:].reshape(B, S, N_HEADS, V_DIM)
    return gate_b, u, q, k, v


def causal_conv(u_ext, w):
    L = u_ext.shape[1] - (CONV_W - 1)
    return sum(w[j] * u_ext[:, j:j + L] for j in range(CONV_W))


def diff_attn_block(q, q_pos, k, v, k_pos, lam):
    s = jnp.einsum('bqhcd,bkhcd->bhcqk', q * (QK_DIM ** -0.5), k).astype(jnp.float32)
    mask = (k_pos[None, :] // CHUNK) <= (q_pos[:, None] // CHUNK)
    s = jnp.where(mask, s, -jnp.inf)
    p = jax.nn.softmax(s, axis=-1)
    a = p[:, :, 0] - lam * p[:, :, 1]
    return jnp.einsum('bhqk,bkhd->bqhd', a.astype(v.dtype), v)


def prompt_attention(q, k, v, lam):
    B, S = q.shape[:2]
    nb = S // Q_BLOCK
    pos = jnp.arange(S, dtype=jnp.int32)
    qb = q.reshape(B, nb, Q_BLOCK, N_HEADS, 2, QK_DIM).transpose(1, 0, 2, 3, 4, 5)
    pb = pos.reshape(nb, Q_BLOCK)
    out = lax.map(lambda a: diff_attn_block(a[0], a[1], k, v, pos, lam), (qb, pb))
    return out.transpose(1, 0, 2, 3, 4).reshape(B, S, N_HEADS, V_DIM)


def finish_layer(x, gate_b, conv_out, attn_o, lambda_init, subln_g, w_out,
                 ln1_g, ln1_b, w_ff1, w_ff2, ln2_g, ln2_b):
    B, S, _ = x.shape
    y_conv = gate_b * conv_out
    attn_o = rms_norm(attn_o, subln_g) * (1.0 - lambda_init)
    mixed = jnp.concatenate([y_conv, attn_o.reshape(B, S, ATTN_DIM)], axis=-1)
    x1 = layer_norm(ALPHA * x + jnp.einsum('bse,ed->bsd', mixed, w_out), ln1_g, ln1_b)
    hdn = jnp.square(jax.nn.relu(jnp.einsum('bsd,df->bsf', x1, w_ff1)))
    return layer_norm(ALPHA * x1 + jnp.einsum('bsf,fd->bsd', hdn, w_ff2), ln2_g, ln2_b)


def setup_inputs(seed: int = 0) -> dict:
    key = jax.random.key(seed)
    ks = jax.random.split(key, 20)
    nrm = lambda k, shp: jax.random.normal(k, shp, jnp.float32)
    col_scale = jnp.concatenate([jnp.ones((IN_WIDTH - V_COLS,), jnp.float32),
                                 jnp.full((V_COLS,), BETA, jnp.float32)])
    return {
        "x_prompt": nrm(ks[0], (BATCH, SEQ, D_MODEL)),
        "x_sample": nrm(ks[1], (DEC_BATCH, DEC_SEQ, D_MODEL)),
        "cache_k": nrm(ks[2], (DEPTH, DEC_BATCH, PAST_LEN, N_HEADS, 2 * QK_DIM)),
        "cache_v": BETA * nrm(ks[3], (DEPTH, DEC_BATCH, PAST_LEN, N_HEADS, V_DIM)),
        "state_conv": nrm(ks[4], (DEPTH, DEC_BATCH, CONV_W - 1, CONV_DIM)),
        "w_in": nrm(ks[5], (DEPTH, D_MODEL, IN_WIDTH)) * (D_MODEL ** -0.5) * col_scale,
        "conv_w": nrm(ks[6], (DEPTH, CONV_W, CONV_DIM)) * (CONV_W ** -0.5),
        "lambda_q1": 0.1 * nrm(ks[7], (DEPTH, QK_DIM)),
        "lambda_k1": 0.1 * nrm(ks[8], (DEPTH, QK_DIM)),
        "lambda_q2": 0.1 * nrm(ks[9], (DEPTH, QK_DIM)),
        "lambda_k2": 0.1 * nrm(ks[10], (DEPTH, QK_DIM)),
        "subln_g": 1.0 + 0.01 * nrm(ks[11], (DEPTH, V_DIM)),
        "w_out": nrm(ks[12], (DEPTH, MIX_WIDTH, D_MODEL)) * (MIX_WIDTH ** -0.5) * BETA,
        "ln1_g": 1.0 + 0.01 * nrm(ks[13], (DEPTH, D_MODEL)),
        "ln1_b": 0.01 * nrm(ks[14], (DEPTH, D_MODEL)),
        "w_ff1": nrm(ks[15], (DEPTH, D_MODEL, D_FF)) * (D_MODEL ** -0.5) * BETA,
        "w_ff2": nrm(ks[16], (DEPTH, D_FF, D_MODEL)) * (D_FF ** -0.5) * BETA,
        "ln2_g": 1.0 + 0.01 * nrm(ks[17], (DEPTH, D_MODEL)),
        "ln2_b": 0.01 * nrm(ks[18], (DEPTH, D_MODEL)),
    }


def reference(x_prompt, x_sample, cache_k, cache_v, state_conv, w_in, conv_w,
              lambda_q1, lambda_k1, lambda_q2, lambda_k2, subln_g, w_out,
              ln1_g, ln1_b, w_ff1, w_ff2, ln2_g, ln2_b):
    xp, xs = x_prompt, x_sample
    Bp, Sp = xp.shape[:2]
    Bs, Ss = xs.shape[:2]
    kp_l, vp_l, cp_l, ks_l, vs_l, cs_l = [], [], [], [], [], []
    for l in range(DEPTH):
        lambda_init = 0.8 - 0.6 * float(np.exp(-0.3 * l))
        lam = (jnp.exp(jnp.sum(lambda_q1[l].astype(jnp.float32) * lambda_k1[l].astype(jnp.float32)))
               - jnp.exp(jnp.sum(lambda_q2[l].astype(jnp.float32) * lambda_k2[l].astype(jnp.float32)))
               + lambda_init)
        lp = (lambda_init, subln_g[l], w_out[l], ln1_g[l], ln1_b[l], w_ff1[l], w_ff2[l], ln2_g[l], ln2_b[l])

        gb, u, q, k, v = project(xp, w_in[l])
        u_ext = jnp.concatenate([jnp.zeros((Bp, CONV_W - 1, CONV_DIM), u.dtype), u], axis=1)
        conv_out = causal_conv(u_ext, conv_w[l])
        attn_o = prompt_attention(q, k, v, lam)
        kp_l.append(k.reshape(Bp, Sp, N_HEADS, 2 * QK_DIM))
        vp_l.append(v)
        cp_l.append(u_ext[:, -(CONV_W - 1):])
        xp = finish_layer(xp, gb, conv_out, attn_o, *lp)

        gb, u, q, k, v = project(xs, w_in[l])
        u_ext = jnp.concatenate([state_conv[l].astype(u.dtype), u], axis=1)
        conv_out = causal_conv(u_ext, conv_w[l])
        k_all = jnp.concatenate(
            [cache_k[l].reshape(Bs, PAST_LEN, N_HEADS, 2, QK_DIM).astype(k.dtype), k], axis=1)
        v_all = jnp.concatenate([cache_v[l].astype(v.dtype), v], axis=1)
        q_pos = PAST_LEN + jnp.arange(Ss, dtype=jnp.int32)
        k_pos = jnp.arange(PAST_LEN + Ss, dtype=jnp.int32)
        attn_o = diff_attn_block(q, q_pos, k_all, v_all, k_pos, lam)
        ks_l.append(k.reshape(Bs, Ss, N_HEADS, 2 * QK_DIM))
        vs_l.append(v)
        cs_l.append(u_ext[:, -(CONV_W - 1):])
        xs = finish_layer(xs, gb, conv_out, attn_o, *lp)

    k_prompt = jnp.stack(kp_l, 0)
    v_prompt = jnp.stack(vp_l, 0)
    conv_prompt = jnp.stack(cp_l, 0)
    k_sample = jnp.stack(ks_l, 0)
    v_sample = jnp.stack(vs_l, 0)
    conv_sample = jnp.stack(cs_l, 0)
    return (xp, xs, k_prompt, v_prompt, conv_prompt, k_sample, v_sample, conv_sample)
```

```python
import os
import numpy as np
import concourse.bass as bass
import concourse.mybir as mybir
from concourse.bass_utils import run_bass_kernel_spmd

F32 = mybir.dt.float32
BF16 = mybir.dt.bfloat16
AF = mybir.ActivationFunctionType
ALU = mybir.AluOpType
AX = mybir.AxisListType

NCORES = 8
SEQ = 16384
D = 1024
NT = 16
NKT = 128
ALPHA = float(2.0 ** 0.25)
EPS = 1e-5
LAMBDA_INIT = 0.2
ONE_M_LI = 1.0 - LAMBDA_INIT
NEG = -30000.0
SEM_LIM = 20000
PHASE_LIMIT = int(os.environ.get('MK_PHASE_LIMIT', '9'))
SUB = int(os.environ.get('MK_SUB', '99'))

ENGS = ("pe", "act", "dve", "pool", "sp")


class _Op:
    __slots__ = ("eng", "fn", "deps", "sig", "dma_key", "sem", "val", "waits")


class Sched:
    def __init__(self):
        self.ops = {e: [] for e in ENGS}
        self.lastw = {}
        self.readers = {}
        self.last_op = {}
        self.dmas = []
        self.dma_queue = {}
        self.ps_acc = {}

    def add(self, eng, fn, reads=(), writes=(), dma_key=None, extra_deps=()):
        o = _Op()
        o.eng = eng
        o.fn = fn
        o.sig = False
        o.dma_key = dma_key
        deps = {}
        ps_r = [r for r in reads if isinstance(r, tuple) and r[0] == "ps"]
        ps_w = [r for r in writes if isinstance(r, tuple) and r[0] == "ps"]
        reads = [r for r in reads if not (isinstance(r, tuple) and r[0] == "ps")]
        writes = [r for r in writes if not (isinstance(r, tuple) and r[0] == "ps")]
        for r, isw in [(x, False) for x in ps_r] + [(x, True) for x in ps_w]:
            acc = self.ps_acc.setdefault(r, {})
            for e2, (o2, w2) in acc.items():
                if e2 != eng or isw or w2:
                    deps[id(o2)] = o2
        for r in reads:
            w = self.lastw.get(r)
            if w is not None:
                deps[id(w)] = w
        for r in writes:
            w = self.lastw.get(r)
            if w is not None:
                deps[id(w)] = w
            for rd in self.readers.get(r, ()):
                deps[id(rd)] = rd
        for d in extra_deps:
            deps[id(d)] = d
        o.deps = [d for d in deps.values()
                  if not (eng == "pe" and d.eng == "pe" and d.dma_key is None and dma_key is None)]
        for r in reads:
            lst = self.readers.setdefault(r, [])
            if dma_key is None:
                lst[:] = [x for x in lst if not (x.eng == eng and x.dma_key is None)]
            lst.append(o)
        for r in writes:
            self.lastw[r] = o
            self.readers[r] = []
        for r, isw in [(x, False) for x in ps_r] + [(x, True) for x in ps_w]:
            prev = self.ps_acc[r].get(eng)
            self.ps_acc[r][eng] = (o, isw or (prev is not None and prev[1] and prev[0] is o))
        for d in o.deps:
            d.sig = True
        self.ops[eng].append(o)
        if dma_key is None:
            if fn is not None:
                self.last_op[eng] = o
        else:
            o.sig = True
            self.dmas.append(o)
            q = self.dma_queue.setdefault(dma_key, eng)
            assert q == eng, (dma_key, q, eng)
        return o

    def barrier(self):
        deps = list(self.last_op.values()) + list(self.dmas)
        self.dmas = []
        for e in ENGS:
            self.add(e, None, extra_deps=deps)

    def finalize(self):
        sem_names = []
        dma_cnt = {}
        for e in ENGS:
            cnt = 0
            for o in self.ops[e]:
                if o.dma_key is not None:
                    k = ("dma", o.dma_key)
                    dma_cnt[k] = dma_cnt.get(k, 0) + 16
                    o.sem = k
                    o.val = dma_cnt[k]
                    if k not in sem_names:
                        sem_names.append(k)
                elif o.sig:
                    ep = cnt // SEM_LIM
                    o.sem = ("eng", e, ep)
                    o.val = cnt % SEM_LIM + 1
                    cnt += 1
                    if o.sem not in sem_names:
                        sem_names.append(o.sem)
        for e in ENGS:
            for o in self.ops[e]:
                w = {}
                for d in o.deps:
                    if w.get(d.sem, 0) < d.val:
                        w[d.sem] = d.val
                o.waits = w
        self.final = dict(dma_cnt)
        return sem_names

    def emit_engine(self, eng_name, e, sems):
        seen = {}
        for o in self.ops[eng_name]:
            for k, v in o.waits.items():
                if seen.get(k, 0) < v:
                    e.wait_ge(sems[k], v)
                    seen[k] = v
            if o.fn is None:
                continue
            inst = o.fn(e)
            if o.sig:
                inst.then_inc(sems[o.sem], 16 if o.dma_key is not None else 1)
        if eng_name == "sp":
            for k, v in self.final.items():
                if seen.get(k, 0) < v:
                    e.wait_ge(sems[k], v)


class SBAlloc:
    def __init__(self, nc):
        self.nc = nc
        self.base = 16512
        self.top = 229344
        self.cur = self.base
        self.top_cur = self.top
        self.n = 0

    def alloc(self, shape, dt, name="t"):
        sz = 1
        for s in shape[1:]:
            sz *= s
        sz *= 4 if dt == F32 else 2
        sz = (sz + 31) // 32 * 32
        off = self.cur
        self.cur += sz
        assert self.cur <= self.top, ("SBUF overflow", name, self.cur - self.base)
        self.n += 1
        return self.nc.alloc_sbuf_tensor_at(f"{name}_{self.n}", list(shape), dt, offset=off)

    def alloc_top(self, shape, dt, name="t"):
        sz = 1
        for s in shape[1:]:
            sz *= s
        sz *= 4 if dt == F32 else 2
        sz = (sz + 31) // 32 * 32
        self.top_cur -= sz
        self.n += 1
        return self.nc.alloc_sbuf_tensor_at(f"{name}_{self.n}", list(shape), dt, offset=self.top_cur)

    def mark(self):
        return self.cur

    def reset(self, m):
        self.cur = m


def build_program():
    nc = bass.Bass("TRN2", target_bir_lowering=False)
    S = Sched()
    sb = SBAlloc(nc)

    def emit():
        sem_names = S.finalize()
        sems = {k: nc.alloc_semaphore(name=f"s{i}") for i, k in enumerate(sem_names)}
        with nc.Block() as block:
            @block.tensor
            def _(e):
                S.emit_engine("pe", e, sems)

            @block.scalar
            def _(e):
                S.emit_engine("act", e, sems)

            @block.vector
            def _(e):
                S.emit_engine("dve", e, sems)

            @block.gpsimd
            def _(e):
                S.emit_engine("pool", e, sems)

            @block.sync
            def _(e):
                S.emit_engine("sp", e, sems)
        return nc

    def din(name, shape, dt=F32):
        return nc.dram_tensor(name, list(shape), dt, kind="ExternalInput").ap()

    def dout(name, shape, dt=F32):
        return nc.dram_tensor(name, list(shape), dt, kind="ExternalOutput").ap()

    def dint(name, shape, dt):
        return nc.dram_tensor(name, list(shape), dt, kind="Internal").ap()

    xT_all = din("xT_all", [D, SEQ if PHASE_LIMIT >= 3 else 512])
    xT_own = din("xT_own", [D, 2048])
    x_own = din("x_own", [2048, D])
    xT_halo = din("xT_halo", [D, 32])
    w_in = din("w_in", [D, 3072])
    w_out = din("w_out", [D, D])
    w_ff1 = din("w_ff1", [D, 4096])
    w_ff2 = din("w_ff2", [4096, D])
    lnp = din("lnp", [128, 4, D])
    subg = din("subg", [128, 128])
    lamv = din("lamv", [128, 4, 64])
    convw = din("convw", [128, 4, 3])
    cid = din("cid", [128, 1])
    ident_d = din("ident", [128, 128])
    xsT = din("xsT", [D, 64])
    xs_d = din("xs", [64, D])
    kcT = din("kcT", [16 if PHASE_LIMIT >= 1 else 1, 128, 2048])
    vc = din("vc", [16 if PHASE_LIMIT >= 1 else 1, 128, 16, 128])
    stT = din("stT", [128, 4, 4, 2])

    y_own = dout("y_own", [2048, D])
    ys_o = dout("ys", [64, D])
    k_own = dout("k_own", [2048, 512])
    v_own = dout("v_own", [2048, 512])
    conv_p = dout("conv_p", [128, 4, 2])
    ks_o = dout("ks", [64, 512])
    vs_o = dout("vs", [64, 512])
    conv_s = dout("conv_s", [128, 4, 4, 2])

    kt_s = dint("kt_s", [4, 128, SEQ], BF16)
    v_s = dint("v_s", [4, 128, NKT * 128], BF16)
    x1_s = dint("x1_s", [2048 + 64, D], F32)
    x1T_s = dint("x1T_s", [D, 2048 + 64], BF16)

    psA = [nc.alloc_psum_tensor(f"psA{i}", [128, 2, 512], F32) for i in range(2)]
    psB = [nc.alloc_psum_tensor(f"psB{i}", [128, 512], F32) for i in range(3)]
    psT = nc.alloc_psum_tensor("psT", [128, 1024], BF16)

    def bank(k):
        return psA[k // 2][:, k % 2, :] if k < 4 else psB[k - 4][:, :]

    def bankres(k):
        return ("ps", k)

    bank_rr = [0]

    def next_bank():
        k = bank_rr[0] % 7
        bank_rr[0] += 1
        return k

    def dma(q, out, in_, key, reads=(), writes=()):
        return S.add(q, lambda e, out=out, in_=in_: e.dma_start(out=out, in_=in_),
                     reads=reads, writes=writes, dma_key=key)

    def mm(out, lhsT, rhs, start, stop, reads, writes, skip=False):
        if skip:
            fn = lambda e: e.matmul(out, lhsT, rhs, start=start, stop=stop, skip_group_check=True)
        else:
            fn = lambda e: e.matmul(out, lhsT, rhs, start=start, stop=stop)
        return S.add("pe", fn, reads=reads, writes=writes)

    def act(func, out, in_, reads, writes, bias=None, scale=None):
        kw = {}
        if bias is not None:
            kw["bias"] = bias
        if scale is not None:
            kw["scale"] = scale
        return S.add("act", lambda e: e.activation(out, in_, func, **kw), reads=reads, writes=writes)

    def ts(eng, out, in0, s1, s2, op0, op1, reads, writes):
        if op1 is None:
            fn = lambda e: e.tensor_scalar(out, in0, s1, None, op0)
        else:
            fn = lambda e: e.tensor_scalar(out, in0, s1, s2, op0, op1)
        return S.add(eng, fn, reads=reads, writes=writes)

    def tt(eng, out, in0, in1, op, reads, writes):
        return S.add(eng, lambda e: e.tensor_tensor(out, in0, in1, op), reads=reads, writes=writes)

    def stt(out, in0, scalar, in1, op0, op1, reads, writes):
        return S.add("dve", lambda e: e.scalar_tensor_tensor(out, in0, scalar, in1, op0, op1),
                     reads=reads, writes=writes)

    def cp(eng, out, in_, reads, writes):
        if eng == "act":
            return S.add("act", lambda e: e.activation(out, in_, AF.Copy), reads=reads, writes=writes)
        return S.add(eng, lambda e: e.tensor_copy(out, in_), reads=reads, writes=writes)

    def memset(eng, ap, val, writes):
        return S.add(eng, lambda e: e.memset(ap, val), writes=writes)

    ident_f = sb.alloc([128, 128], F32, "identf")
    ident_b = sb.alloc([128, 128], BF16, "identb")
    g08 = sb.alloc([128, 128], F32, "g08")
    lamv_t = sb.alloc([128, 4, 64], F32, "lamv")
    convw_t = sb.alloc([128, 4, 3], F32, "convw")
    small = sb.alloc([128, 128], F32, "small")
    junk = sb.alloc([128, 64], F32, "junk")
    m_consts = sb.mark()
    mixedT = sb.alloc([128, 8, 2048], BF16, "mixedT")
    mixedT_s = sb.alloc([128, 8, 64], BF16, "mixedTs")
    m_g2 = sb.mark()
    QT = sb.alloc([128, 4, 2048], BF16, "QT")
    EPSC, DMASK, LAM, CIDC, E1, E2, S1, S2 = 0, 1, 2, 3, 4, 5, 6, 7
    RV0 = 8
    KM0 = 16

    def sc(c, p=128):
        return small[0:p, c:c + 1]

    dma("sp", ident_f[:, :], ident_d, "identf", writes=["identf"])
    dma("sp", g08[:, :], subg, "g08", writes=["g08"])
    dma("sp", lamv_t[:, :, :], lamv, "lamv", writes=["lamv"])
    dma("sp", convw_t[:, :, :], convw, "convw", writes=["convw"])
    dma("sp", sc(CIDC), cid, "cid", writes=["cid"])
    cp("dve", ident_b[:, :], ident_f[:, :], ["identf"], ["identb"])
    ts("dve", g08[:, :], g08[:, :], ONE_M_LI, None, ALU.mult, None, ["g08"], ["g08"])
    memset("dve", sc(EPSC), EPS, ["eps"])
    memset("dve", small[0:64, DMASK:DMASK + 1], 0.0, ["dmask"])
    memset("dve", small[64:128, DMASK:DMASK + 1], NEG, ["dmask"])
    for r in range(8):
        memset("dve", sc(RV0 + r), float(r), ["rv"])
    ts("dve", small[:, KM0:KM0 + 8], small[:, RV0:RV0 + 8], sc(CIDC), 8.0, ALU.add, ALU.is_ge,
       ["rv", "cid"], ["kmask"])
    ts("dve", small[:, KM0:KM0 + 8], small[:, KM0:KM0 + 8], -1.0, -NEG, ALU.add, ALU.mult,
       ["kmask"], ["kmask"])
    tt("dve", junk[:, 0:64], lamv_t[:, 0, :], lamv_t[:, 1, :], ALU.mult, ["lamv"], ["junk"])
    S.add("dve", lambda e: e.reduce_sum(sc(S1), junk[:, 0:64], AX.X), reads=["junk"], writes=["s1"])
    tt("dve", junk[:, 0:64], lamv_t[:, 2, :], lamv_t[:, 3, :], ALU.mult, ["lamv"], ["junk"])
    S.add("dve", lambda e: e.reduce_sum(sc(S2), junk[:, 0:64], AX.X), reads=["junk"], writes=["s2"])
    act(AF.Exp, sc(E1), sc(S1), ["s1"], ["e1"])
    act(AF.Exp, sc(E2), sc(S2), ["s2"], ["e2"])
    tt("dve", sc(LAM), sc(E1), sc(E2), ALU.subtract, ["e1", "e2"], ["lam"])
    ts("dve", sc(LAM), sc(LAM), LAMBDA_INIT, None, ALU.add, None, ["lam"], ["lam"])

    m_global = sb.mark()
    if PHASE_LIMIT < -1:
        return emit()

    def ln_head(P, r, rres, scol):
        st = small[0:P, scol:scol + 12]
        mv = small[0:P, scol + 12:scol + 14]
        t1 = small[0:P, scol + 14:scol + 15]
        rstd = small[0:P, scol + 15:scol + 16]
        sres = ("lnsmall", scol)
        S.add("dve", lambda e: e.bn_stats(st[:, 0:6], r[:, 0:512]), reads=[rres], writes=[sres])
        S.add("dve", lambda e: e.bn_stats(st[:, 6:12], r[:, 512:1024]), reads=[rres, sres], writes=[sres])
        S.add("dve", lambda e: e.bn_aggr(mv, st), reads=[sres], writes=[sres])
        act(AF.Ln, t1, mv[:, 1:2], [sres, "eps"], [("lnt1", scol)], bias=sc(EPSC, P))
        act(AF.Exp, rstd, t1, [("lnt1", scol)], [("lnrstd", scol)], scale=-0.5)
        m1 = small[0:P, scol + 17:scol + 18]
        nb = small[0:P, scol + 16:scol + 17]
        act(AF.Copy, m1, mv[:, 0:1], [sres, ("lnrstd", scol)], [("lnm1", scol)], scale=rstd)
        S.add("act", lambda e: e.mul(nb, m1, -1.0), reads=[("lnm1", scol)], writes=[("lnnb", scol)])

    def ln_tail(P, r, rres, out, outres, gt, tmp, tmpres, scol):
        rstd = small[0:P, scol + 15:scol + 16]
        nb = small[0:P, scol + 16:scol + 17]
        act(AF.Identity, tmp, r, [rres, ("lnrstd", scol), ("lnnb", scol)], [tmpres], bias=nb, scale=rstd)
        tt("dve", tmp, tmp, gt[0][0:P, 0, :], ALU.mult, [tmpres, gt[1]], [tmpres])
        tt("dve", out, tmp, gt[0][0:P, 1, :], ALU.add, [tmpres, gt[1]], [outres])

    def layer_norm(P, r, rres, out, outres, gt, tmp, tmpres, scol):
        ln_head(P, r, rres, scol)
        ln_tail(P, r, rres, out, outres, gt, tmp, tmpres, scol)

    w_in_t = sb.alloc([128, 8, 3072], BF16, "w_in")
    dma("pool", w_in_t[:, :, :], w_in.rearrange("(k p) n -> p k n", p=128), "w_in", writes=["w_in"])
    m_A = sb.mark()

    if PHASE_LIMIT < 0:
        return emit()
    xsT_t = sb.alloc([128, 8, 64], BF16, "xsT")
    gate_s = sb.alloc([128, 4, 64], F32, "gate_s")
    C_s = sb.alloc([128, 4, 64], F32, "C_s")
    uext_s = sb.alloc([128, 4, 4, 18], F32, "uext_s")
    tconv_s = sb.alloc([128, 4, 64], F32, "tconv_s")
    Qbd = sb.alloc([128, 4, 4, 32], BF16, "Qbd")
    KTn = sb.alloc([128, 4, 64], BF16, "KTn")
    Vn = sb.alloc([16, 4, 4, 129], BF16, "Vn")
    ksst = sb.alloc([64, 512], F32, "ksst")
    vsst = sb.alloc([16, 4, 512], F32, "vsst")

    dma("pool", xsT_t[:, :, :], xsT.rearrange("(k p) n -> p k n", p=128), "xsT", writes=["xsT"])
    st_stage = sb.alloc([128, 4, 4, 2], F32, "st_stage")
    cs_stage = sb.alloc([128, 4, 4, 2], F32, "cs_stage")
    dma("sp", st_stage[:, :, :, :], stT, "st_stage", writes=["st_stage"])
    cp("dve", uext_s[:, :, :, 0:2], st_stage[:, :, :, :], ["st_stage"], ["uext_s_st"])
    memset("dve", Qbd[:, :, :, :], 0.0, ["Qbd"])
    memset("dve", Vn[:, :, :, 128:129], 1.0, ["Vn1"])

    if SUB < -2:
        return emit()
    for grp in range(3):
        if (SUB == -2 and grp == 1) or (SUB == -1 and grp == 2):
            return emit()
        chunks = list(range(grp * 8, min(grp * 8 + 8, 20)))
        k = next_bank()
        for ci, ch in enumerate(chunks):
            for kc in range(8):
                mm(bank(k)[:, ci * 64:(ci + 1) * 64], w_in_t[:, kc, ch * 128:(ch + 1) * 128],
                   xsT_t[:, kc, :], kc == 0, kc == 7, ["w_in", "xsT"], [bankres(k)])
        bk = bank(k)
        if grp == 0:
            cp("act", gate_s[:, :, :], bk[:, 0:256].rearrange("p (c n) -> p c n", n=64),
               [bankres(k)], ["gate_s"])
            cp("act", C_s[:, :, :], bk[:, 256:512].rearrange("p (c n) -> p c n", n=64),
               [bankres(k)], ["C_s"])
        elif grp == 1:
            for ch in range(4):
                tt("dve", uext_s[:, ch, :, 2:18],
                   C_s[:, ch, :].rearrange("p (b t) -> p b t", t=16),
                   bk[:, ch * 64:(ch + 1) * 64].rearrange("p (b t) -> p b t", t=16), ALU.mult,
                   [bankres(k), "C_s"], ["uext_s"])
            for h in range(4):
                src = bk[:, 256 + h * 64:256 + (h + 1) * 64].rearrange("p (b t) -> p b t", t=16)
                S.add("act", lambda e, h=h, src=src: e.mul(Qbd[0:64, :, h, 0:16], src[0:64], 0.125),
                      reads=[bankres(k)], writes=["Qbd"])
                S.add("act", lambda e, h=h, src=src: e.mul(Qbd[64:128, :, h, 16:32], src[64:128], 0.125),
                      reads=[bankres(k)], writes=["Qbd"])
        else:
            cp("dve", KTn[:, :, :], bk[:, 0:256].rearrange("p (h n) -> p h n", n=64),
               [bankres(k)], ["KTn"])
    if SUB < 1:
        return emit()
    k = next_bank()
    for kc in range(8):
        mm(bank(k)[0:64, :], xsT_t[:, kc, :], w_in_t[:, kc, 2048:2560], kc == 0, kc == 7,
           ["w_in", "xsT"], [bankres(k)])
    cp("dve", ksst[:, :], bank(k)[0:64, :], [bankres(k)], ["ksst"])
    dma("sp", ks_o, ksst[:, :], "ksst", reads=["ksst"])
    if SUB < 2:
        return emit()
    for b in range(4):
        k = next_bank()
        for kc in range(8):
            mm(bank(k)[0:16, :], xsT_t[:, kc, b * 16:(b + 1) * 16], w_in_t[:, kc, 2560:3072],
               kc == 0, kc == 7, ["w_in", "xsT"], [bankres(k)])
        cp("dve", vsst[:, b, :], bank(k)[0:16, :], [bankres(k)], ["vsst"])
        cp("act", Vn[:, b, :, 0:128], bank(k)[0:16, :].rearrange("p (h d) -> p h d", d=128),
           [bankres(k)], ["Vn"])
    dma("sp", vs_o.rearrange("(b t) c -> t b c", t=16), vsst[:, :, :], "vsst", reads=["vsst"])
    if SUB < 3:
        return emit()
    cp("dve", cs_stage[:, :, :, :], uext_s[:, :, :, 16:18], ["uext_s", "uext_s_st"], ["cs_stage"])
    dma("sp", conv_s, cs_stage[:, :, :, :], "convs_o", reads=["cs_stage"])
    for ch in range(4):
        tv = tconv_s[:, ch, :].rearrange("p (b t) -> p b t", t=16)
        rr = ["uext_s", "uext_s_st", "convw"]
        ts("dve", tv, uext_s[:, ch, :, 0:16], convw_t[:, ch, 0:1], None, ALU.mult, None, rr, ["tconv_s"])
        stt(tv, uext_s[:, ch, :, 1:17], convw_t[:, ch, 1:2], tv, ALU.mult, ALU.add, rr + ["tconv_s"], ["tconv_s"])
        stt(tv, uext_s[:, ch, :, 2:18], convw_t[:, ch, 2:3], tv, ALU.mult, ALU.add, rr + ["tconv_s"], ["tconv_s"])
        tt("dve", mixedT_s[:, ch, :], tconv_s[:, ch, :], gate_s[:, ch, :], ALU.mult,
           ["tconv_s", "gate_s"], ["mixedTs"])

    if PHASE_LIMIT < 1:
        return emit()
    KTc = [sb.alloc([128, 2048], BF16, f"KTc{i}") for i in range(2)]
    Vc = [sb.alloc([128, 16, 128], BF16, f"Vc{i}") for i in range(2)]
    ones_c = sb.alloc([128, 2], BF16, "ones_c")
    memset("dve", ones_c[:, :], 1.0, ["ones_c"])
    Pc = [sb.alloc([128, 16, 32], BF16, f"Pc{i}") for i in range(2)]
    Pn = [sb.alloc([16, 32], BF16, f"Pn{i}") for i in range(2)]
    ep_s = sb.alloc([16, 512], F32, "ep_s")
    attn_s = sb.alloc([16, 128], BF16, "attn_s")

    def epilogue(P, O0, O1, ores, work, wres, attn_out, ares, scol):
        rl = small[0:P, scol:scol + 2]
        ssq = small[0:P, scol + 2:scol + 3]
        t1 = small[0:P, scol + 3:scol + 4]
        rstd = small[0:P, scol + 4:scol + 5]
        sres = ("epsmall", scol)
        a1 = work[0:P, 0:128]
        a = work[0:P, 128:256]
        sq = work[0:P, 256:384]
        ores = list(ores)
        S.add("dve", lambda e: e.reciprocal(rl[:, 0:1], O0[:, 128:129]), reads=ores, writes=[sres])
        S.add("dve", lambda e: e.reciprocal(rl[:, 1:2], O1[:, 128:129]), reads=ores + [sres], writes=[sres])
        ts("dve", a1, O1[:, 0:128], rl[:, 1:2], sc(LAM, P), ALU.mult, ALU.mult, ores + [sres, "lam"], [wres])
        stt(a, O0[:, 0:128], rl[:, 0:1], a1, ALU.mult, ALU.subtract, ores + [sres, wres], [wres])
        tt("dve", sq, a, a, ALU.mult, [wres], [wres])
        S.add("dve", lambda e: e.reduce_sum(ssq, sq, AX.X), reads=[wres, sres], writes=[sres])
        act(AF.Ln, t1, ssq, [sres, "eps"], [sres], bias=sc(EPSC, P), scale=1.0 / 128.0)
        act(AF.Exp, rstd, t1, [sres], [sres], scale=-0.5)
        stt(attn_out, a, rstd, g08[0:P, :], ALU.mult, ALU.mult, [wres, sres, "g08"], [ares])

    for bh in range(16):
        b, h = bh // 4, bh % 4
        sl = bh % 2
        dma("pool", KTc[sl][:, :], kcT[bh], ("KTc", sl), writes=[("KTc", sl)])
        dma("pool", Vc[sl][:, :, :], vc[bh], ("Vc", sl), writes=[("Vc", sl)])
        ks_ = sl
        km = 2 + sl
        sres, mres = bankres(ks_), bankres(km)
        for t in range(16):
            mm(bank(ks_)[:, t * 32:(t + 1) * 32], KTc[sl][:, t * 128:(t + 1) * 128], Qbd[:, b, h, :],
               True, True, [("KTc", sl), "Qbd"], [sres])
        mm(bank(km)[0:16, 258:290], KTn[:, h, b * 16:(b + 1) * 16], Qbd[:, b, h, :], True, True,
           ["KTn", "Qbd"], [mres])
        act(AF.Exp, Pc[sl][:, :, :], bank(ks_).rearrange("p (t n) -> p t n", n=32), [sres], [("Pc", sl)])
        act(AF.Exp, Pn[sl][:, :], bank(km)[0:16, 258:290], [mres], [("Pn", sl)])
        for c in range(2):
            Oc = bank(km)[0:16, c * 129:(c + 1) * 129]
            for t in range(16):
                mm(Oc[:, 0:128], Pc[sl][:, t, c * 16:(c + 1) * 16], Vc[sl][:, t, :], t == 0, False,
                   [("Pc", sl), ("Vc", sl)], [mres], skip=True)
                mm(Oc[:, 128:129], Pc[sl][:, t, c * 16:(c + 1) * 16], ones_c[:, 0:1], False, False,
                   [("Pc", sl), "ones_c"], [mres], skip=True)
            mm(Oc, Pn[sl][:, c * 16:(c + 1) * 16], Vn[:, b, h, :], False, True,
               [("Pn", sl), "Vn", "Vn1"], [mres], skip=True)
        epilogue(16, bank(km)[0:16, 0:129], bank(km)[0:16, 129:258], [mres], ep_s, "ep_s",
                 attn_s[:, :], "attn_s", 24)
        S.add("pe", lambda e, bh=bh: e.transpose(psT[:, bh * 16:(bh + 1) * 16], attn_s[:, :], ident_b[0:16, 0:16]),
              reads=["attn_s", "identb"], writes=[("ps", 7)])
    for h in range(4):
        cp("dve", mixedT_s[:, 4 + h, :].rearrange("p (b t) -> p b t", t=16),
           psT[:, 0:256].rearrange("p (b h t) -> p h b t", h=4, t=16)[:, h, :, :],
           [("ps", 7)], ["mixedTs"])

    sb.reset(m_A)
    S.barrier()

    if PHASE_LIMIT < 2:
        return emit()
    xTo = [sb.alloc([128, 8, 512], BF16, f"xTo{i}") for i in range(2)]
    xTh = sb.alloc([128, 8, 32], BF16, "xTh")
    C_h = sb.alloc([128, 4, 32], F32, "C_h")
    u_halo = sb.alloc([128, 4, 16, 2], F32, "u_halo")
    gate_g = sb.alloc([128, 4, 512], F32, "gate_g")
    C_g = sb.alloc([128, 4, 512], F32, "C_g")
    u_g = sb.alloc([128, 4, 4, 130], F32, "u_g")
    tconv = sb.alloc([128, 512], F32, "tconv")
    cp_stage = sb.alloc([128, 4, 2], F32, "cp_stage")
    kvst = [sb.alloc([128, 1024], F32, f"kvst{i}") for i in range(2)]

    dma("pool", xTh[:, :, :], xT_halo.rearrange("(k p) n -> p k n", p=128), "xTh", writes=["xTh"])
    k = next_bank()
    for ci in range(8):
        ch = 4 + ci
        for kc in range(8):
            mm(bank(k)[:, ci * 32:(ci + 1) * 32], w_in_t[:, kc, ch * 128:(ch + 1) * 128], xTh[:, kc, :],
               kc == 0, kc == 7, ["w_in", "xTh"], [bankres(k)])
    cp("act", C_h[:, :, :], bank(k)[:, 0:128].rearrange("p (c n) -> p c n", n=32), [bankres(k)], ["C_h"])
    for ch in range(4):
        tt("dve", u_halo[:, ch, :, :], C_h[:, ch, :].rearrange("p (j t) -> p j t", t=2),
           bank(k)[:, 128 + ch * 32:128 + (ch + 1) * 32].rearrange("p (j t) -> p j t", t=2), ALU.mult,
           [bankres(k), "C_h"], ["u_halo"])

    xTo_v = xT_own.rearrange("(k p) n -> p k n", p=128)
    dma("pool", xTo[0][:, :, :], xTo_v[:, :, 0:512], ("xTo", 0), writes=[("xTo", 0)])
    kv_i = 0
    for g in range(4):
        sl = g % 2
        if g + 1 < 4:
            dma("pool", xTo[1 - sl][:, :, :], xTo_v[:, :, (g + 1) * 512:(g + 2) * 512], ("xTo", 1 - sl),
                writes=[("xTo", 1 - sl)])
        xr = ("xTo", sl)
        cp("dve", u_g[:, :, :, 0:2], u_halo[:, :, 4 * g:4 * g + 4, :], ["u_halo"], ["u_g_h"])
        for ch in range(16):
            k = next_bank()
            for kc in range(8):
                mm(bank(k), w_in_t[:, kc, ch * 128:(ch + 1) * 128], xTo[sl][:, kc, :], kc == 0, kc == 7,
                   ["w_in", xr], [bankres(k)])
            if ch < 4:
                cp("act", gate_g[:, ch, :], bank(k), [bankres(k)], [("gate_g", ch)])
            elif ch < 8:
                cp("act", C_g[:, ch - 4, :], bank(k), [bankres(k)], [("C_g", ch - 4)])
            elif ch < 12:
                c4 = ch - 8
                tt("dve", u_g[:, c4, :, 2:130], C_g[:, c4, :].rearrange("p (j t) -> p j t", t=128),
                   bank(k).rearrange("p (j t) -> p j t", t=128), ALU.mult,
                   [bankres(k), ("C_g", c4)], [("u_g", c4)])
                tv = tconv[:, :].rearrange("p (j t) -> p j t", t=128)
                rr = [("u_g", c4), "u_g_h", "convw"]
                ts("dve", tv, u_g[:, c4, :, 0:128], convw_t[:, c4, 0:1], None, ALU.mult, None, rr, ["tconv"])
                stt(tv, u_g[:, c4, :, 1:129], convw_t[:, c4, 1:2], tv, ALU.mult, ALU.add, rr + ["tconv"], ["tconv"])
                stt(tv, u_g[:, c4, :, 2:130], convw_t[:, c4, 2:3], tv, ALU.mult, ALU.add, rr + ["tconv"], ["tconv"])
                tt("dve", mixedT[:, c4, g * 512:(g + 1) * 512], tconv[:, :], gate_g[:, c4, :], ALU.mult,
                   ["tconv", ("gate_g", c4)], [("mixedT", c4, g)])
            else:
                h = ch - 12
                S.add("act", lambda e, h=h, k=k, g=g: e.mul(QT[:, h, g * 512:(g + 1) * 512], bank(k), 0.125),
                      reads=[bankres(k)], writes=[("QT", h, g)])
        if g == 3:
            cp("dve", cp_stage[:, :, :], u_g[:, :, 3, 128:130], [("u_g", c) for c in range(4)], ["cp_stage"])
            dma("sp", conv_p, cp_stage[:, :, :], "convp_o", reads=["cp_stage"])
        for tl in range(4):
            st_i = kv_i % 2
            kv_i += 1
            for half in range(2):
                k = next_bank()
                for kc in range(8):
                    mm(bank(k), xTo[sl][:, kc, tl * 128:(tl + 1) * 128],
                       w_in_t[:, kc, 2048 + half * 512:2048 + (half + 1) * 512], kc == 0, kc == 7,
                       ["w_in", xr], [bankres(k)])
                cp("act" if half == 0 else "dve", kvst[st_i][:, half * 512:(half + 1) * 512], bank(k),
                   [bankres(k)], [("kvst", st_i, half)])
            row = (g * 4 + tl) * 128
            dma("sp", k_own[row:row + 128, :], kvst[st_i][:, 0:512], ("kvst", st_i, 0), reads=[("kvst", st_i, 0)])
            dma("sp", v_own[row:row + 128, :], kvst[st_i][:, 512:1024], ("kvst", st_i, 1), reads=[("kvst", st_i, 1)])

    if PHASE_LIMIT < 3:
        return emit()
    m_A4 = sb.mark()
    xTa = [sb.alloc([128, 8, 512], BF16, f"xTa{i}") for i in range(3)]
    ktst = [sb.alloc([128, 4, 512], BF16, f"ktst{i}") for i in range(2)]
    vst = [sb.alloc([128, 4, 4, 128], BF16, f"vst{i}") for i in range(2)]
    xTa_v = xT_all.rearrange("(k p) n -> p k n", p=128)
    kt_v = kt_s.rearrange("h p n -> p h n")
    v_v = v_s.rearrange("h t n -> t h n")
    NG = 32
    for gi in range(2):
        dma("pool", xTa[gi][:, :, :], xTa_v[:, :, gi * 512:(gi + 1) * 512], ("xTa", gi), writes=[("xTa", gi)])
    for gi in range(NG):
        sl = gi % 3
        if gi + 2 < NG:
            s2 = (gi + 2) % 3
            dma("pool", xTa[s2][:, :, :], xTa_v[:, :, (gi + 2) * 512:(gi + 3) * 512], ("xTa", s2),
                writes=[("xTa", s2)])
        xr = ("xTa", sl)
        st_i = gi % 2
        for h in range(4):
            k = next_bank()
            for kc in range(8):
                mm(bank(k), w_in_t[:, kc, 2048 + h * 128:2048 + (h + 1) * 128], xTa[sl][:, kc, :],
                   kc == 0, kc == 7, ["w_in", xr], [bankres(k)])
            cp("act" if h % 2 == 0 else "dve", ktst[st_i][:, h, :], bank(k), [bankres(k)], [("ktst", st_i)])
        dma("sp", kt_v[:, :, gi * 512:(gi + 1) * 512], ktst[st_i][:, :, :], ("ktst", st_i), reads=[("ktst", st_i)])
        for tl in range(4):
            k = next_bank()
            for kc in range(8):
                mm(bank(k), xTa[sl][:, kc, tl * 128:(tl + 1) * 128], w_in_t[:, kc, 2560:3072],
                   kc == 0, kc == 7, ["w_in", xr], [bankres(k)])
            cp("act" if tl % 2 == 1 else "dve", vst[st_i][:, :, tl, :],
               bank(k).rearrange("p (h d) -> p h d", d=128), [bankres(k)], [("vst", st_i)])
        dma("sp", v_v[:, :, gi * 512:(gi + 1) * 512],
            vst[st_i][:, :, :, :].rearrange("p h t d -> p h (t d)"), ("vst", st_i), reads=[("vst", st_i)])

    sb.reset(m_global)
    S.barrier()

    if PHASE_LIMIT < 4:
        return emit()
    if PHASE_LIMIT < 4:
        return emit()
    off_kt0 = sb.cur
    KT, Vb = [], []
    for i in range(2):
        KT.append(sb.alloc([128, SEQ], BF16, f"KT{i}"))
        Vb.append(sb.alloc([128, NKT, 129], BF16, f"Vb{i}"))
    w_ff1_t = nc.alloc_sbuf_tensor_at("w_ff1_t", [128, 8, 4096], BF16, offset=off_kt0)
    w1v = w_ff1.rearrange("(k p) n -> p k n", p=128)
    Pb = [sb.alloc([128, 2, 512], BF16, f"Pb{i}") for i in range(3)]
    epw = [sb.alloc([128, 384], F32, f"epw{i}") for i in range(2)]
    attn_t = [sb.alloc([128, 128], BF16, f"attn{i}") for i in range(2)]
    Ocp = sb.alloc([128, 8, 129], F32, "Ocp")
    a4 = sb.alloc([128, 4, 128], F32, "a4")
    for i in range(2):
        memset("dve", Vb[i][:, :, 128:129], 1.0, [("Vb1", i)])

    def load_head(h):
        hb = h % 2
        for q in range(4):
            dma("sp", KT[hb][:, q * 4096:(q + 1) * 4096], kt_s[h, :, q * 4096:(q + 1) * 4096],
                ("KT", hb, q), writes=[("KT", hb, q)])
            dma("sp", Vb[hb][:, 32 * q:32 * (q + 1), 0:128],
                v_s[h, :, q * 4096:(q + 1) * 4096].rearrange("p (k d) -> p k d", d=128),
                ("Vb", hb, q), writes=[("Vb", hb, q)])

    def Oacc(idx):
        return psB[idx // 3][:, (idx % 3) * 129:(idx % 3) * 129 + 129]

    deferred = []
    ep_cnt = [0]
    unit_cnt = [0]

    def flush_deferred(u=None):
        keep = []
        for (tu, f) in deferred:
            if u is None or tu == u:
                f()
            else:
                keep.append((tu, f))
        deferred[:] = keep

    load_head(0)
    for h in range(4):
        hb = h % 2
        if h + 1 < 4:
            load_head(h + 1)
        if h == 3:
            for kc2 in range(4):
                dma("pool", w_ff1_t[:, 2 * kc2:2 * kc2 + 2, :], w1v[:, 2 * kc2:2 * kc2 + 2, :], ("w_ff1", kc2),
                    writes=[("w_ff1", kc2), ("Vb1", 0)] + [("KT", 0, q) for q in range(4)]
                    + [("Vb", 0, q) for q in range(4)])
        for G in range(4):
            units = [(jp, rp) for jp in range(4 * G + 4) for rp in range(8)]

            def emit_S(u, jp, rp):
                kidx = jp * 8 + rp
                q = jp // 4
                i0 = max(0, jp - 4 * G)
                c0 = i0 * 128
                sbf = u % 2
                for c in range(2):
                    mm(psA[sbf][c * 64:(c + 1) * 64, c, c0:512] if False else psA[sbf][:, c, c0:512],
                       KT[hb][c * 64:(c + 1) * 64, kidx * 128:(kidx + 1) * 128],
                       QT[c * 64:(c + 1) * 64, h, G * 512 + c0:(G + 1) * 512],
                       True, True, [("KT", hb, q), ("QT", h, G)], [("ps", 2 * sbf + c)])

            def emit_exp(u, jp, rp):
                i0 = max(0, jp - 4 * G)
                c0 = i0 * 128
                sbf, pbf = u % 2, u % 3
                Sx, Px = psA[sbf], Pb[pbf]
                rS = [("ps", 2 * sbf), ("ps", 2 * sbf + 1)]
                if jp < 4 * G:
                    act(AF.Exp, Px[:, :, 0:512], Sx[:, :, 0:512], rS, [("P", pbf, 0)])
                else:
                    if rp == 0:
                        act(AF.Exp, Px[:, :, c0:c0 + 64], Sx[:, :, c0:c0 + 64], rS + ["dmask"],
                            [("P", pbf, 0)], bias=sc(DMASK))
                        act(AF.Exp, Px[:, :, c0 + 64:c0 + 128], Sx[:, :, c0 + 64:c0 + 128], rS,
                            [("P", pbf, 1)])
                    else:
                        act(AF.Exp, Px[:, :, c0:c0 + 128], Sx[:, :, c0:c0 + 128], rS + ["kmask"],
                            [("P", pbf, 0)], bias=sc(KM0 + rp))
                    if i0 < 3:
                        act(AF.Exp, Px[:, :, c0 + 128:512], Sx[:, :, c0 + 128:512], rS, [("P", pbf, 2)])

            def emit_AV(u, jp, rp):
                kidx = jp * 8 + rp
                q = jp // 4
                i0 = max(0, jp - 4 * G)
                pbf = u % 3
                for i in range(i0, 4):
                    for c in range(2):
                        idx = i * 2 + c
                        first = kidx == 0
                        last = kidx == 8 * (4 * G + i) + 7
                        mm(Oacc(idx), Pb[pbf][:, c, i * 128:(i + 1) * 128], Vb[hb][:, kidx, :],
                           first and (idx % 3 == 0), last,
                           [("P", pbf, 0), ("P", pbf, 1), ("P", pbf, 2), ("Vb", hb, q), ("Vb1", hb)],
                           [("ps", 4 + idx // 3)], skip=True)

            nU = len(units)
            emit_S(0, *units[0])
            emit_S(1, *units[1])
            for u in range(nU):
                emit_exp(u, *units[u])
                if u + 2 < nU:
                    emit_S(u + 2, *units[u + 2])
                emit_AV(u, *units[u])
                if deferred:
                    flush_deferred(u)
            for bk in range(3):
                n = 3 if bk < 2 else 2
                cp("dve", Ocp[:, 3 * bk:3 * bk + n, :],
                   psB[bk][:, 0:n * 129].rearrange("p (a d) -> p a d", d=129), [("ps", 4 + bk)], [("Ocp", bk)])

            def stage1(h=h, G=G):
                for i in range(4):
                    O0, O1 = Ocp[:, 2 * i, :], Ocp[:, 2 * i + 1, :]
                    ores = sorted({("Ocp", (2 * i) // 3), ("Ocp", (2 * i + 1) // 3)})
                    rl = small[:, 80 + 2 * i:82 + 2 * i]
                    a1 = epw[0][:, 0:128]
                    sq = epw[0][:, 128:256]
                    a = a4[:, i, :]
                    S.add("dve", lambda e, rl=rl, O0=O0: e.reciprocal(rl[:, 0:1], O0[:, 128:129]), reads=ores, writes=[("rl", i)])
                    S.add("dve", lambda e, rl=rl, O1=O1: e.reciprocal(rl[:, 1:2], O1[:, 128:129]), reads=ores + [("rl", i)], writes=[("rl", i)])
                    ts("dve", a1, O1[:, 0:128], rl[:, 1:2], sc(LAM), ALU.mult, ALU.mult, ores + [("rl", i), "lam"], ["epw_a1"])
                    stt(a, O0[:, 0:128], rl[:, 0:1], a1, ALU.mult, ALU.subtract, ores + [("rl", i), "epw_a1"], [("a4", i)])
                    tt("dve", sq, a, a, ALU.mult, [("a4", i)], ["epw_sq"])
                    S.add("dve", lambda e, i=i, sq=sq: e.reduce_sum(small[:, 96 + i:97 + i], sq, AX.X), reads=["epw_sq"], writes=[("ssq", i)])

            def stage2():
                act(AF.Ln, small[:, 100:104], small[:, 96:100], [("ssq", i) for i in range(4)] + ["eps"], ["ept"],
                    bias=sc(EPSC), scale=1.0 / 128.0)
                act(AF.Exp, small[:, 104:108], small[:, 100:104], ["ept"], ["eprstd"], scale=-0.5)

            def stage3(h=h, G=G):
                for i in range(4):
                    e_i = ep_cnt[0] % 2
                    ep_cnt[0] += 1
                    stt(attn_t[e_i][:, :], a4[:, i, :], small[:, 104 + i:105 + i], g08[:, :], ALU.mult, ALU.mult,
                        [("a4", i), "eprstd", "g08"], [("attn", e_i)])
                    S.add("pe", lambda e, i=i, e_i=e_i: e.transpose(psT[:, i * 128:(i + 1) * 128], attn_t[e_i][:, :], ident_b[:, :]),
                          reads=[("attn", e_i), "identb"], writes=[("ps", 7)])
                    cp("dve", mixedT[:, 4 + h, (4 * G + i) * 128:(4 * G + i + 1) * 128],
                       psT[:, i * 128:(i + 1) * 128], [("ps", 7)], [("mixedT", 4 + h, G, i)])

            deferred.append((1, stage1))
            deferred.append((4, stage2))
            deferred.append((7, stage3))
    flush_deferred()

    w_out_t = nc.alloc_sbuf_tensor_at("w_out_t", [128, 8, D], BF16, offset=m_g2)
    dma("pool", w_out_t[:, :, :], w_out.rearrange("(k p) n -> p k n", p=128), "w_out",
        writes=["w_out"] + [("QT", h, G) for h in range(4) for G in range(4)])

    sb.reset(m_g2)
    S.barrier()

    if PHASE_LIMIT < 5:
        return emit()
    sb.cur = m_g2 + 128 * 8 * 2 * 8
    assert sb.cur <= off_kt0, (sb.cur, off_kt0)
    sb.cur = off_kt0 + 65536
    w2a = sb.alloc_top([128, 16, D], BF16, "w2a")
    w2v = w_ff2.rearrange("(k p) n -> p k n", p=128)
    dma("pool", w2a[:, :, :], w2v[:, 0:16, :], "w2a", writes=["w2a"])
    xt = [sb.alloc([128, D], F32, f"xt{i}") for i in range(2)]
    rt = [sb.alloc([128, D], F32, f"rt{i}") for i in range(2)]
    tmpt0 = sb.alloc([128, D], F32, "tmpt")
    tmpt = [tmpt0, tmpt0]
    ln1_t = sb.alloc([128, 2, D], F32, "ln1")
    dma("sp", ln1_t[:, :, :], lnp[:, 0:2, :], "ln1", writes=["ln1"])
    x1t = [sb.alloc([128, D], F32, f"x1t{i}") for i in range(2)]
    x1Tst = [sb.alloc([128, 8, 128], BF16, f"x1Tst{i}") for i in range(2)]
    x1Tv = x1T_s.rearrange("(k p) n -> p k n", p=128)
    assert sb.cur <= sb.top_cur, (sb.cur, sb.top_cur)

    def c_outproj(tile):
        P = 128 if tile < 16 else 64
        r0 = tile * 128
        sl = tile % 2
        src = x_own[r0:r0 + 128, :] if tile < 16 else xs_d
        dma("sp", xt[sl][0:P, :], src, ("xt", sl), writes=[("xt", sl)])
        pa = psA[sl]
        for half in range(2):
            for mc in range(8):
                lhs = mixedT[:, mc, r0:r0 + 128] if tile < 16 else mixedT_s[:, mc, :]
                rd = ["w_out"]
                if tile < 16:
                    if mc < 4:
                        rd.append(("mixedT", mc, tile // 4))
                    else:
                        rd.append(("mixedT", mc, tile // 4, tile % 4))
                else:
                    rd.append("mixedTs")
                mm(pa[0:P, half, :], lhs, w_out_t[:, mc, half * 512:(half + 1) * 512], mc == 0, mc == 7,
                   rd, [("ps", 2 * sl + half)])

    def c_head(tile):
        P = 128 if tile < 16 else 64
        sl = tile % 2
        pa = psA[sl]
        stt(rt[sl][0:P, :], xt[sl][0:P, :], ALPHA, pa[0:P, :, :].rearrange("p a n -> p (a n)"), ALU.mult, ALU.add,
            [("xt", sl), ("ps", 2 * sl), ("ps", 2 * sl + 1)], [("rt", sl)])
        ln_head(P, rt[sl][0:P, :], ("rt", sl), 40 + 20 * sl)

    def c_post(tile):
        P = 128 if tile < 16 else 64
        r0 = tile * 128
        sl = tile % 2
        pa = psA[sl]
        ln_tail(P, rt[sl][0:P, :], ("rt", sl), x1t[sl][0:P, :], ("x1t", sl), (ln1_t, "ln1"), tmpt[sl][0:P, :],
                ("tmpt", 0), 40 + 20 * sl)
        dma("pool", x1_s[r0:r0 + P, :], x1t[sl][0:P, :], ("x1t", sl), reads=[("x1t", sl)], writes=[("x1s", tile)])
        for kc in range(8):
            pb_ = psB[kc // 4]
            S.add("pe", lambda e, kc=kc, pb_=pb_, P=P, sl=sl: e.transpose(
                pb_[:, (kc % 4) * 128:(kc % 4) * 128 + P], x1t[sl][0:P, kc * 128:(kc + 1) * 128], ident_f[0:P, 0:P]),
                reads=[("x1t", sl), "identf"], writes=[("ps", 4 + kc // 4)])
        for hf in range(2):
            cp("act", x1Tst[sl][:, 4 * hf:4 * hf + 4, 0:P],
               psB[hf][:, :].rearrange("p (k n) -> p k n", n=128)[:, :, 0:P], [("ps", 4 + hf)], [("x1Tst", sl)])
        dma("pool", x1Tv[:, :, r0:r0 + P], x1Tst[sl][:, :, 0:P], ("x1Tst", sl), reads=[("x1Tst", sl)],
            writes=[("x1Ts", tile)])

    c_outproj(0)
    c_outproj(1)
    c_head(0)
    for tile in range(17):
        if tile + 1 < 17:
            c_head(tile + 1)
        if tile + 2 < 17:
            c_outproj(tile + 2)
        c_post(tile)

    sb.reset(m_consts)
    S.barrier()

    if PHASE_LIMIT < 6:
        return emit()
    w2b = sb.alloc([128, 16, D], BF16, "w2b")
    dma("pool", w2b[:, :, :], w2v[:, 16:32, :], "w2b", writes=["w2b"])
    x1Tg0 = sb.alloc([128, 8, 512], BF16, "x1Tg0")
    ln2_t = sb.alloc([128, 2, D], F32, "ln2")
    dma("sp", ln2_t[:, :, :], lnp[:, 2:4, :], "ln2", writes=["ln2"])
    assert sb.cur <= off_kt0, (sb.cur, off_kt0)
    sb.cur = off_kt0 + 65536
    x1Tg = [x1Tg0, sb.alloc([128, 8, 512], BF16, "x1Tg1")]
    hT = sb.alloc([128, 32, 512], BF16, "hT")
    x1g = sb.alloc([128, D], F32, "x1g")
    rt2 = sb.alloc([128, D], F32, "rt2")
    tmp2 = sb.alloc([128, D], F32, "tmp2")
    yst = sb.alloc([128, D], F32, "yst")
    relu_t = sb.alloc([128, 512], F32, "relu")
    assert sb.cur <= sb.top_cur, (sb.cur, sb.top_cur)

    NGRP = 5

    gorder = [0, 1, 2, 4, 3]

    def d_load(oi):
        g2 = gorder[oi]
        N2 = 512 if g2 < 4 else 64
        t2 = [4 * g2 + k_ for k_ in range(4)] if g2 < 4 else [16]
        dma("sp", x1Tg[oi % 2][:, :, 0:N2], x1Tv[:, :, g2 * 512:g2 * 512 + N2], ("x1Tg", oi % 2),
            reads=[("x1Ts", t) for t in t2], writes=[("x1Tg", oi % 2)])

    for oi, grp in enumerate(gorder):
        N = 512 if grp < 4 else 64
        sl = oi % 2
        tiles = [4 * grp + k_ for k_ in range(4)] if grp < 4 else [16]
        for o2 in ([0, 1] if oi == 0 else [oi + 1]):
            if o2 < NGRP:
                d_load(o2)
        for fc in range(32):
            kb = fc % 3
            for kc in range(8):
                mm(psB[kb][:, 0:N], w_ff1_t[:, kc, fc * 128:(fc + 1) * 128], x1Tg[sl][:, kc, 0:N],
                   kc == 0, kc == 7, [("w_ff1", kc // 2), ("x1Tg", sl)], [("ps", 4 + kb)])
            act(AF.Relu, relu_t[:, 0:N], psB[kb][:, 0:N], [("ps", 4 + kb)], ["relu"])
            tt("dve", hT[:, fc, 0:N], relu_t[:, 0:N], relu_t[:, 0:N], ALU.mult, ["relu"], [("hT", fc)])
        for ti, tile in enumerate(tiles):
            P = 128 if tile < 16 else 64
            r0 = tile * 128
            s2 = tile % 2
            dma("sp", x1g[0:P, :], x1_s[r0:r0 + P, :], "x1g", reads=[("x1s", tile)], writes=["x1g"])
            pa = psA[s2]
            for half in range(2):
                for fc in range(32):
                    w2 = w2a if fc < 16 else w2b
                    mm(pa[0:P, half, :], hT[:, fc, ti * 128:ti * 128 + P],
                       w2[:, fc % 16, half * 512:(half + 1) * 512], fc == 0, fc == 31,
                       [("hT", fc), "w2a" if fc < 16 else "w2b"], [("ps", 2 * s2 + half)])
            stt(rt2[0:P, :], x1g[0:P, :], ALPHA, pa[0:P, :, :].rearrange("p a n -> p (a n)"),
                ALU.mult, ALU.add, ["x1g", ("ps", 2 * s2), ("ps", 2 * s2 + 1)], ["rt2"])
            layer_norm(P, rt2[0:P, :], "rt2", yst[0:P, :], "yst", (ln2_t, "ln2"), tmp2[0:P, :], "tmp2", 40)
            dst = y_own[r0:r0 + 128, :] if tile < 16 else ys_o
            dma("pool", dst, yst[0:P, :], "yst", reads=["yst"])

    return emit()


_NC_CACHE = {}


def _get_nc():
    if "nc" not in _NC_CACHE:
        _NC_CACHE["nc"] = build_program()
    return _NC_CACHE["nc"]


def kernel(x_prompt, x_sample, cache_k, cache_v, state_conv, w_in, conv_w,
           lambda_q1, lambda_k1, lambda_q2, lambda_k2, subln_g, w_out,
           ln1_g, ln1_b, w_ff1, w_ff2, ln2_g, ln2_b):
    f32 = np.float32
    X = np.asarray(x_prompt, f32)[0]
    Xt = X.reshape(128, 128, D)
    xs_all = np.asarray(x_sample, f32)
    ck = np.asarray(cache_k, f32)[0]
    cv = np.asarray(cache_v, f32)[0]
    stc = np.asarray(state_conv, f32)[0]
    lnp = np.broadcast_to(np.stack([ln1_g[0], ln1_b[0], ln2_g[0], ln2_b[0]]).astype(f32)[None], (128, 4, D))
    subg = np.broadcast_to(np.asarray(subln_g, f32)[0][None, :], (128, 128))
    lamv = np.broadcast_to(np.stack([lambda_q1[0], lambda_k1[0], lambda_q2[0], lambda_k2[0]]).astype(f32)[None],
                           (128, 4, 64))
    convw = np.asarray(conv_w, f32)[0].T.reshape(4, 128, 3).transpose(1, 0, 2)
    shared = {
        "w_in": np.ascontiguousarray(w_in[0], f32), "w_out": np.ascontiguousarray(w_out[0], f32),
        "w_ff1": np.ascontiguousarray(w_ff1[0], f32), "w_ff2": np.ascontiguousarray(w_ff2[0], f32),
        "lnp": np.ascontiguousarray(lnp), "subg": np.ascontiguousarray(subg),
        "lamv": np.ascontiguousarray(lamv), "convw": np.ascontiguousarray(convw),
        "ident": np.eye(128, dtype=f32),
    }
    in_maps = []
    for c in range(NCORES):
        order = [8 * j + (r + c) % 8 for j in range(16) for r in range(8)]
        own = [8 * j + c for j in range(16)]
        x_own = Xt[own].reshape(2048, D)
        halo = np.zeros((16, 2, D), f32)
        for j in range(16):
            s0 = own[j] * 128
            if s0 >= 2:
                halo[j] = X[s0 - 2:s0]
        xs = xs_all[4 * c:4 * c + 4].reshape(64, D)
        kcT = ck[4 * c:4 * c + 4].transpose(0, 2, 3, 1).reshape(16, 128, 2048)
        vcl = cv[4 * c:4 * c + 4].reshape(4, 16, 128, 4, 128).transpose(0, 3, 2, 1, 4).reshape(16, 128, 16, 128)
        stT = stc[4 * c:4 * c + 4].reshape(4, 2, 4, 128).transpose(3, 2, 0, 1)
        m = dict(shared)
        m.update({
            "xT_all": np.ascontiguousarray(Xt[order].reshape(SEQ, D).T) if PHASE_LIMIT >= 3 else np.ascontiguousarray(X[0:512].T),
            "xT_own": np.ascontiguousarray(x_own.T),
            "x_own": np.ascontiguousarray(x_own),
            "xT_halo": np.ascontiguousarray(halo.reshape(32, D).T),
            "cid": np.full((128, 1), float(c), f32),
            "xsT": np.ascontiguousarray(xs.T), "xs": np.ascontiguousarray(xs),
            "kcT": np.ascontiguousarray(kcT if PHASE_LIMIT >= 1 else kcT[0:1]),
            "vc": np.ascontiguousarray(vcl if PHASE_LIMIT >= 1 else vcl[0:1]),
            "stT": np.ascontiguousarray(stT),
        })
        in_maps.append(m)
    nc = _get_nc()
    res = run_bass_kernel_spmd(nc, in_maps, core_ids=list(range(NCORES)))
    R = res.results
    y_prompt = np.zeros((128, 128, D), f32)
    k_prompt = np.zeros((128, 128, 512), f32)
    v_prompt = np.zeros((128, 128, 512), f32)
    y_sample = np.zeros((32, 16, D), f32)
    k_sample = np.zeros((32, 16, 512), f32)
    v_sample = np.zeros((32, 16, 512), f32)
    conv_sample = np.zeros((32, 2, 512), f32)
    for c in range(NCORES):
        own = [8 * j + c for j in range(16)]
        y_prompt[own] = R[c]["y_own"].reshape(16, 128, D)
        k_prompt[own] = R[c]["k_own"].reshape(16, 128, 512)
        v_prompt[own] = R[c]["v_own"].reshape(16, 128, 512)
        y_sample[4 * c:4 * c + 4] = R[c]["ys"].reshape(4, 16, D)
        k_sample[4 * c:4 * c + 4] = R[c]["ks"].reshape(4, 16, 512)
        v_sample[4 * c:4 * c + 4] = R[c]["vs"].reshape(4, 16, 512)
        conv_sample[4 * c:4 * c + 4] = R[c]["conv_s"].transpose(2, 3, 1, 0).reshape(4, 2, 512)
    conv_prompt = R[7]["conv_p"].transpose(2, 1, 0).reshape(1, 1, 2, 512)
    return (y_prompt.reshape(1, SEQ, D), y_sample,
            k_prompt.reshape(1, 1, SEQ, 4, 128), v_prompt.reshape(1, 1, SEQ, 4, 128),
            np.ascontiguousarray(conv_prompt),
            k_sample.reshape(1, 32, 16, 4, 128), v_sample.reshape(1, 32, 16, 4, 128),
            conv_sample.reshape(1, 32, 2, 512))
```

```python
import os
import numpy as np
import concourse.bass as bass
import concourse.mybir as mybir
from concourse.bass_utils import run_bass_kernel_spmd

F32 = mybir.dt.float32
BF16 = mybir.dt.bfloat16
AF = mybir.ActivationFunctionType
ALU = mybir.AluOpType
AX = mybir.AxisListType

NCORES = 8
SEQ = 16384
D = 1024
NT = 16
NKT = 128
ALPHA = float(2.0 ** 0.25)
EPS = 1e-5
LAMBDA_INIT = 0.2
ONE_M_LI = 1.0 - LAMBDA_INIT
NEG = -30000.0
SEM_LIM = 20000
PHASE_LIMIT = int(os.environ.get('MK_PHASE_LIMIT', '9'))
SUB = int(os.environ.get('MK_SUB', '99'))

ENGS = ("pe", "act", "dve", "pool", "sp")


class _Op:
    __slots__ = ("eng", "fn", "deps", "sig", "dma_key", "sem", "val", "waits")


class Sched:
    def __init__(self):
        self.ops = {e: [] for e in ENGS}
        self.lastw = {}
        self.readers = {}
        self.last_op = {}
        self.dmas = []
        self.dma_queue = {}
        self.ps_acc = {}

    def add(self, eng, fn, reads=(), writes=(), dma_key=None, extra_deps=()):
        o = _Op()
        o.eng = eng
        o.fn = fn
        o.sig = False
        o.dma_key = dma_key
        deps = {}
        ps_r = [r for r in reads if isinstance(r, tuple) and r[0] == "ps"]
        ps_w = [r for r in writes if isinstance(r, tuple) and r[0] == "ps"]
        reads = [r for r in reads if not (isinstance(r, tuple) and r[0] == "ps")]
        writes = [r for r in writes if not (isinstance(r, tuple) and r[0] == "ps")]
        for r, isw in [(x, False) for x in ps_r] + [(x, True) for x in ps_w]:
            acc = self.ps_acc.setdefault(r, {})
            for e2, (o2, w2) in acc.items():
                if e2 != eng or isw or w2:
                    deps[id(o2)] = o2
        for r in reads:
            w = self.lastw.get(r)
            if w is not None:
                deps[id(w)] = w
        for r in writes:
            w = self.lastw.get(r)
            if w is not None:
                deps[id(w)] = w
            for rd in self.readers.get(r, ()):
                deps[id(rd)] = rd
        for d in extra_deps:
            deps[id(d)] = d
        o.deps = [d for d in deps.values()
                  if not (eng == "pe" and d.eng == "pe" and d.dma_key is None and dma_key is None)]
        for r in reads:
            lst = self.readers.setdefault(r, [])
            if dma_key is None:
                lst[:] = [x for x in lst if not (x.eng == eng and x.dma_key is None)]
            lst.append(o)
        for r in writes:
            self.lastw[r] = o
            self.readers[r] = []
        for r, isw in [(x, False) for x in ps_r] + [(x, True) for x in ps_w]:
            prev = self.ps_acc[r].get(eng)
            self.ps_acc[r][eng] = (o, isw or (prev is not None and prev[1] and prev[0] is o))
        for d in o.deps:
            d.sig = True
        self.ops[eng].append(o)
        if dma_key is None:
            if fn is not None:
                self.last_op[eng] = o
        else:
            o.sig = True
            self.dmas.append(o)
            q = self.dma_queue.setdefault(dma_key, eng)
            assert q == eng, (dma_key, q, eng)
        return o

    def barrier(self):
        deps = list(self.last_op.values()) + list(self.dmas)
        self.dmas = []
        for e in ENGS:
            self.add(e, None, extra_deps=deps)

    def finalize(self):
        sem_names = []
        dma_cnt = {}
        for e in ENGS:
            cnt = 0
            for o in self.ops[e]:
                if o.dma_key is not None:
                    k = ("dma", o.dma_key)
                    dma_cnt[k] = dma_cnt.get(k, 0) + 16
                    o.sem = k
                    o.val = dma_cnt[k]
                    if k not in sem_names:
                        sem_names.append(k)
                elif o.sig:
                    ep = cnt // SEM_LIM
                    o.sem = ("eng", e, ep)
                    o.val = cnt % SEM_LIM + 1
                    cnt += 1
                    if o.sem not in sem_names:
                        sem_names.append(o.sem)
        for e in ENGS:
            for o in self.ops[e]:
                w = {}
                for d in o.deps:
                    if w.get(d.sem, 0) < d.val:
                        w[d.sem] = d.val
                o.waits = w
        self.final = dict(dma_cnt)
        return sem_names

    def emit_engine(self, eng_name, e, sems):
        seen = {}
        for o in self.ops[eng_name]:
            for k, v in o.waits.items():
                if seen.get(k, 0) < v:
                    e.wait_ge(sems[k], v)
                    seen[k] = v
            if o.fn is None:
                continue
            inst = o.fn(e)
            if o.sig:
                inst.then_inc(sems[o.sem], 16 if o.dma_key is not None else 1)
        if eng_name == "sp":
            for k, v in self.final.items():
                if seen.get(k, 0) < v:
                    e.wait_ge(sems[k], v)


class SBAlloc:
    def __init__(self, nc):
        self.nc = nc
        self.base = 16512
        self.top = 229344
        self.cur = self.base
        self.top_cur = self.top
        self.n = 0

    def alloc(self, shape, dt, name="t"):
        sz = 1
        for s in shape[1:]:
            sz *= s
        sz *= 4 if dt == F32 else 2
        sz = (sz + 31) // 32 * 32
        off = self.cur
        self.cur += sz
        assert self.cur <= self.top, ("SBUF overflow", name, self.cur - self.base)
        self.n += 1
        return self.nc.alloc_sbuf_tensor_at(f"{name}_{self.n}", list(shape), dt, offset=off)

    def alloc_top(self, shape, dt, name="t"):
        sz = 1
        for s in shape[1:]:
            sz *= s
        sz *= 4 if dt == F32 else 2
        sz = (sz + 31) // 32 * 32
        self.top_cur -= sz
        self.n += 1
        return self.nc.alloc_sbuf_tensor_at(f"{name}_{self.n}", list(shape), dt, offset=self.top_cur)

    def mark(self):
        return self.cur

    def reset(self, m):
        self.cur = m


def build_program():
    nc = bass.Bass("TRN2", target_bir_lowering=False)
    S = Sched()
    sb = SBAlloc(nc)

    def emit():
        sem_names = S.finalize()
        sems = {k: nc.alloc_semaphore(name=f"s{i}") for i, k in enumerate(sem_names)}
        with nc.Block() as block:
            @block.tensor
            def _(e):
                S.emit_engine("pe", e, sems)

            @block.scalar
            def _(e):
                S.emit_engine("act", e, sems)

            @block.vector
            def _(e):
                S.emit_engine("dve", e, sems)

            @block.gpsimd
            def _(e):
                S.emit_engine("pool", e, sems)

            @block.sync
            def _(e):
                S.emit_engine("sp", e, sems)
        return nc

    def din(name, shape, dt=F32):
        return nc.dram_tensor(name, list(shape), dt, kind="ExternalInput").ap()

    def dout(name, shape, dt=F32):
        return nc.dram_tensor(name, list(shape), dt, kind="ExternalOutput").ap()

    def dint(name, shape, dt):
        return nc.dram_tensor(name, list(shape), dt, kind="Internal").ap()

    xT_all = din("xT_all", [D, SEQ if PHASE_LIMIT >= 3 else 512])
    xT_own = din("xT_own", [D, 2048])
    x_own = din("x_own", [2048, D])
    xT_halo = din("xT_halo", [D, 32])
    w_in = din("w_in", [D, 3072])
    w_out = din("w_out", [D, D])
    w_ff1 = din("w_ff1", [D, 4096])
    w_ff2 = din("w_ff2", [4096, D])
    lnp = din("lnp", [128, 4, D])
    subg = din("subg", [128, 128])
    lamv = din("lamv", [128, 4, 64])
    convw = din("convw", [128, 4, 3])
    cid = din("cid", [128, 1])
    ident_d = din("ident", [128, 128])
    xsT = din("xsT", [D, 64])
    xs_d = din("xs", [64, D])
    kcT = din("kcT", [16 if PHASE_LIMIT >= 1 else 1, 128, 2048])
    vc = din("vc", [16 if PHASE_LIMIT >= 1 else 1, 128, 16, 128])
    stT = din("stT", [128, 4, 4, 2])

    y_own = dout("y_own", [2048, D])
    ys_o = dout("ys", [64, D])
    k_own = dout("k_own", [2048, 512])
    v_own = dout("v_own", [2048, 512])
    conv_p = dout("conv_p", [128, 4, 2])
    ks_o = dout("ks", [64, 512])
    vs_o = dout("vs", [64, 512])
    conv_s = dout("conv_s", [128, 4, 4, 2])

    kt_s = dint("kt_s", [4, 128, SEQ], BF16)
    v_s = dint("v_s", [4, 128, NKT * 128], BF16)
    x1_s = dint("x1_s", [2048 + 64, D], F32)
    x1T_s = dint("x1T_s", [D, 2048 + 64], BF16)

    psA = [nc.alloc_psum_tensor(f"psA{i}", [128, 2, 512], F32) for i in range(2)]
    psB = [nc.alloc_psum_tensor(f"psB{i}", [128, 512], F32) for i in range(3)]
    psT = nc.alloc_psum_tensor("psT", [128, 1024], BF16)

    def bank(k):
        return psA[k // 2][:, k % 2, :] if k < 4 else psB[k - 4][:, :]

    def bankres(k):
        return ("ps", k)

    bank_rr = [0]

    def next_bank():
        k = bank_rr[0] % 7
        bank_rr[0] += 1
        return k

    def dma(q, out, in_, key, reads=(), writes=()):
        return S.add(q, lambda e, out=out, in_=in_: e.dma_start(out=out, in_=in_),
                     reads=reads, writes=writes, dma_key=key)

    def mm(out, lhsT, rhs, start, stop, reads, writes, skip=False):
        if skip:
            fn = lambda e: e.matmul(out, lhsT, rhs, start=start, stop=stop, skip_group_check=True)
        else:
            fn = lambda e: e.matmul(out, lhsT, rhs, start=start, stop=stop)
        return S.add("pe", fn, reads=reads, writes=writes)

    def act(func, out, in_, reads, writes, bias=None, scale=None):
        kw = {}
        if bias is not None:
            kw["bias"] = bias
        if scale is not None:
            kw["scale"] = scale
        return S.add("act", lambda e: e.activation(out, in_, func, **kw), reads=reads, writes=writes)

    def ts(eng, out, in0, s1, s2, op0, op1, reads, writes):
        if op1 is None:
            fn = lambda e: e.tensor_scalar(out, in0, s1, None, op0)
        else:
            fn = lambda e: e.tensor_scalar(out, in0, s1, s2, op0, op1)
        return S.add(eng, fn, reads=reads, writes=writes)

    def tt(eng, out, in0, in1, op, reads, writes):
        return S.add(eng, lambda e: e.tensor_tensor(out, in0, in1, op), reads=reads, writes=writes)

    def stt(out, in0, scalar, in1, op0, op1, reads, writes):
        return S.add("dve", lambda e: e.scalar_tensor_tensor(out, in0, scalar, in1, op0, op1),
                     reads=reads, writes=writes)

    def cp(eng, out, in_, reads, writes):
        if eng == "act":
            return S.add("act", lambda e: e.activation(out, in_, AF.Copy), reads=reads, writes=writes)
        return S.add(eng, lambda e: e.tensor_copy(out, in_), reads=reads, writes=writes)

    def memset(eng, ap, val, writes):
        return S.add(eng, lambda e: e.memset(ap, val), writes=writes)

    ident_f = sb.alloc([128, 128], F32, "identf")
    ident_b = sb.alloc([128, 128], BF16, "identb")
    g08 = sb.alloc([128, 128], F32, "g08")
    lamv_t = sb.alloc([128, 4, 64], F32, "lamv")
    convw_t = sb.alloc([128, 4, 3], F32, "convw")
    small = sb.alloc([128, 128], F32, "small")
    junk = sb.alloc([128, 64], F32, "junk")
    m_consts = sb.mark()
    mixedT = sb.alloc([128, 8, 2048], BF16, "mixedT")
    mixedT_s = sb.alloc([128, 8, 64], BF16, "mixedTs")
    m_g2 = sb.mark()
    QT = sb.alloc([128, 4, 2048], BF16, "QT")
    EPSC, DMASK, LAM, CIDC, E1, E2, S1, S2 = 0, 1, 2, 3, 4, 5, 6, 7
    RV0 = 8
    KM0 = 16

    def sc(c, p=128):
        return small[0:p, c:c + 1]

    dma("sp", ident_f[:, :], ident_d, "identf", writes=["identf"])
    dma("sp", g08[:, :], subg, "g08", writes=["g08"])
    dma("sp", lamv_t[:, :, :], lamv, "lamv", writes=["lamv"])
    dma("sp", convw_t[:, :, :], convw, "convw", writes=["convw"])
    dma("sp", sc(CIDC), cid, "cid", writes=["cid"])
    cp("dve", ident_b[:, :], ident_f[:, :], ["identf"], ["identb"])
    ts("dve", g08[:, :], g08[:, :], ONE_M_LI, None, ALU.mult, None, ["g08"], ["g08"])
    memset("dve", sc(EPSC), EPS, ["eps"])
    memset("dve", small[0:64, DMASK:DMASK + 1], 0.0, ["dmask"])
    memset("dve", small[64:128, DMASK:DMASK + 1], NEG, ["dmask"])
    for r in range(8):
        memset("dve", sc(RV0 + r), float(r), ["rv"])
    ts("dve", small[:, KM0:KM0 + 8], small[:, RV0:RV0 + 8], sc(CIDC), 8.0, ALU.add, ALU.is_ge,
       ["rv", "cid"], ["kmask"])
    ts("dve", small[:, KM0:KM0 + 8], small[:, KM0:KM0 + 8], -1.0, -NEG, ALU.add, ALU.mult,
       ["kmask"], ["kmask"])
    tt("dve", junk[:, 0:64], lamv_t[:, 0, :], lamv_t[:, 1, :], ALU.mult, ["lamv"], ["junk"])
    S.add("dve", lambda e: e.reduce_sum(sc(S1), junk[:, 0:64], AX.X), reads=["junk"], writes=["s1"])
    tt("dve", junk[:, 0:64], lamv_t[:, 2, :], lamv_t[:, 3, :], ALU.mult, ["lamv"], ["junk"])
    S.add("dve", lambda e: e.reduce_sum(sc(S2), junk[:, 0:64], AX.X), reads=["junk"], writes=["s2"])
    act(AF.Exp, sc(E1), sc(S1), ["s1"], ["e1"])
    act(AF.Exp, sc(E2), sc(S2), ["s2"], ["e2"])
    tt("dve", sc(LAM), sc(E1), sc(E2), ALU.subtract, ["e1", "e2"], ["lam"])
    ts("dve", sc(LAM), sc(LAM), LAMBDA_INIT, None, ALU.add, None, ["lam"], ["lam"])

    m_global = sb.mark()
    if PHASE_LIMIT < -1:
        return emit()

    def ln_head(P, r, rres, scol):
        st = small[0:P, scol:scol + 12]
        mv = small[0:P, scol + 12:scol + 14]
        t1 = small[0:P, scol + 14:scol + 15]
        rstd = small[0:P, scol + 15:scol + 16]
        sres = ("lnsmall", scol)
        S.add("dve", lambda e: e.bn_stats(st[:, 0:6], r[:, 0:512]), reads=[rres], writes=[sres])
        S.add("dve", lambda e: e.bn_stats(st[:, 6:12], r[:, 512:1024]), reads=[rres, sres], writes=[sres])
        S.add("dve", lambda e: e.bn_aggr(mv, st), reads=[sres], writes=[sres])
        act(AF.Ln, t1, mv[:, 1:2], [sres, "eps"], [("lnt1", scol)], bias=sc(EPSC, P))
        act(AF.Exp, rstd, t1, [("lnt1", scol)], [("lnrstd", scol)], scale=-0.5)
        m1 = small[0:P, scol + 17:scol + 18]
        nb = small[0:P, scol + 16:scol + 17]
        act(AF.Copy, m1, mv[:, 0:1], [sres, ("lnrstd", scol)], [("lnm1", scol)], scale=rstd)
        S.add("act", lambda e: e.mul(nb, m1, -1.0), reads=[("lnm1", scol)], writes=[("lnnb", scol)])

    def ln_tail(P, r, rres, out, outres, gt, tmp, tmpres, scol):
        rstd = small[0:P, scol + 15:scol + 16]
        nb = small[0:P, scol + 16:scol + 17]
        act(AF.Identity, tmp, r, [rres, ("lnrstd", scol), ("lnnb", scol)], [tmpres], bias=nb, scale=rstd)
        tt("dve", tmp, tmp, gt[0][0:P, 0, :], ALU.mult, [tmpres, gt[1]], [tmpres])
        tt("dve", out, tmp, gt[0][0:P, 1, :], ALU.add, [tmpres, gt[1]], [outres])

    def layer_norm(P, r, rres, out, outres, gt, tmp, tmpres, scol):
        ln_head(P, r, rres, scol)
        ln_tail(P, r, rres, out, outres, gt, tmp, tmpres, scol)

    w_in_t = sb.alloc([128, 8, 3072], BF16, "w_in")
    dma("pool", w_in_t[:, :, :], w_in.rearrange("(k p) n -> p k n", p=128), "w_in", writes=["w_in"])
    m_A = sb.mark()

    if PHASE_LIMIT < 0:
        return emit()
    xsT_t = sb.alloc([128, 8, 64], BF16, "xsT")
    gate_s = sb.alloc([128, 4, 64], F32, "gate_s")
    C_s = sb.alloc([128, 4, 64], F32, "C_s")
    uext_s = sb.alloc([128, 4, 4, 18], F32, "uext_s")
    tconv_s = sb.alloc([128, 4, 64], F32, "tconv_s")
    Qbd = sb.alloc([128, 4, 4, 32], BF16, "Qbd")
    KTn = sb.alloc([128, 4, 64], BF16, "KTn")
    Vn = sb.alloc([16, 4, 4, 129], BF16, "Vn")
    ksst = sb.alloc([64, 512], F32, "ksst")
    vsst = sb.alloc([16, 4, 512], F32, "vsst")

    dma("pool", xsT_t[:, :, :], xsT.rearrange("(k p) n -> p k n", p=128), "xsT", writes=["xsT"])
    st_stage = sb.alloc([128, 4, 4, 2], F32, "st_stage")
    cs_stage = sb.alloc([128, 4, 4, 2], F32, "cs_stage")
    dma("sp", st_stage[:, :, :, :], stT, "st_stage", writes=["st_stage"])
    cp("dve", uext_s[:, :, :, 0:2], st_stage[:, :, :, :], ["st_stage"], ["uext_s_st"])
    memset("dve", Qbd[:, :, :, :], 0.0, ["Qbd"])
    memset("dve", Vn[:, :, :, 128:129], 1.0, ["Vn1"])

    if SUB < -2:
        return emit()
    for grp in range(3):
        if (SUB == -2 and grp == 1) or (SUB == -1 and grp == 2):
            return emit()
        chunks = list(range(grp * 8, min(grp * 8 + 8, 20)))
        k = next_bank()
        for ci, ch in enumerate(chunks):
            for kc in range(8):
                mm(bank(k)[:, ci * 64:(ci + 1) * 64], w_in_t[:, kc, ch * 128:(ch + 1) * 128],
                   xsT_t[:, kc, :], kc == 0, kc == 7, ["w_in", "xsT"], [bankres(k)])
        bk = bank(k)
        if grp == 0:
            cp("act", gate_s[:, :, :], bk[:, 0:256].rearrange("p (c n) -> p c n", n=64),
               [bankres(k)], ["gate_s"])
            cp("act", C_s[:, :, :], bk[:, 256:512].rearrange("p (c n) -> p c n", n=64),
               [bankres(k)], ["C_s"])
        elif grp == 1:
            for ch in range(4):
                tt("dve", uext_s[:, ch, :, 2:18],
                   C_s[:, ch, :].rearrange("p (b t) -> p b t", t=16),
                   bk[:, ch * 64:(ch + 1) * 64].rearrange("p (b t) -> p b t", t=16), ALU.mult,
                   [bankres(k), "C_s"], ["uext_s"])
            for h in range(4):
                src = bk[:, 256 + h * 64:256 + (h + 1) * 64].rearrange("p (b t) -> p b t", t=16)
                S.add("act", lambda e, h=h, src=src: e.mul(Qbd[0:64, :, h, 0:16], src[0:64], 0.125),
                      reads=[bankres(k)], writes=["Qbd"])
                S.add("act", lambda e, h=h, src=src: e.mul(Qbd[64:128, :, h, 16:32], src[64:128], 0.125),
                      reads=[bankres(k)], writes=["Qbd"])
        else:
            cp("dve", KTn[:, :, :], bk[:, 0:256].rearrange("p (h n) -> p h n", n=64),
               [bankres(k)], ["KTn"])
    if SUB < 1:
        return emit()
    k = next_bank()
    for kc in range(8):
        mm(bank(k)[0:64, :], xsT_t[:, kc, :], w_in_t[:, kc, 2048:2560], kc == 0, kc == 7,
           ["w_in", "xsT"], [bankres(k)])
    cp("dve", ksst[:, :], bank(k)[0:64, :], [bankres(k)], ["ksst"])
    dma("sp", ks_o, ksst[:, :], "ksst", reads=["ksst"])
    if SUB < 2:
        return emit()
    for b in range(4):
        k = next_bank()
        for kc in range(8):
            mm(bank(k)[0:16, :], xsT_t[:, kc, b * 16:(b + 1) * 16], w_in_t[:, kc, 2560:3072],
               kc == 0, kc == 7, ["w_in", "xsT"], [bankres(k)])
        cp("dve", vsst[:, b, :], bank(k)[0:16, :], [bankres(k)], ["vsst"])
        cp("act", Vn[:, b, :, 0:128], bank(k)[0:16, :].rearrange("p (h d) -> p h d", d=128),
           [bankres(k)], ["Vn"])
    dma("sp", vs_o.rearrange("(b t) c -> t b c", t=16), vsst[:, :, :], "vsst", reads=["vsst"])
    if SUB < 3:
        return emit()
    cp("dve", cs_stage[:, :, :, :], uext_s[:, :, :, 16:18], ["uext_s", "uext_s_st"], ["cs_stage"])
    dma("sp", conv_s, cs_stage[:, :, :, :], "convs_o", reads=["cs_stage"])
    for ch in range(4):
        tv = tconv_s[:, ch, :].rearrange("p (b t) -> p b t", t=16)
        rr = ["uext_s", "uext_s_st", "convw"]
        ts("dve", tv, uext_s[:, ch, :, 0:16], convw_t[:, ch, 0:1], None, ALU.mult, None, rr, ["tconv_s"])
        stt(tv, uext_s[:, ch, :, 1:17], convw_t[:, ch, 1:2], tv, ALU.mult, ALU.add, rr + ["tconv_s"], ["tconv_s"])
        stt(tv, uext_s[:, ch, :, 2:18], convw_t[:, ch, 2:3], tv, ALU.mult, ALU.add, rr + ["tconv_s"], ["tconv_s"])
        tt("dve", mixedT_s[:, ch, :], tconv_s[:, ch, :], gate_s[:, ch, :], ALU.mult,
           ["tconv_s", "gate_s"], ["mixedTs"])

    if PHASE_LIMIT < 1:
        return emit()
    KTc = [sb.alloc([128, 2048], BF16, f"KTc{i}") for i in range(2)]
    Vc = [sb.alloc([128, 16, 128], BF16, f"Vc{i}") for i in range(2)]
    ones_c = sb.alloc([128, 2], BF16, "ones_c")
    memset("dve", ones_c[:, :], 1.0, ["ones_c"])
    Pc = [sb.alloc([128, 16, 32], BF16, f"Pc{i}") for i in range(2)]
    Pn = [sb.alloc([16, 32], BF16, f"Pn{i}") for i in range(2)]
    ep_s = sb.alloc([16, 512], F32, "ep_s")
    attn_s = sb.alloc([16, 128], BF16, "attn_s")

    def epilogue(P, O0, O1, ores, work, wres, attn_out, ares, scol):
        rl = small[0:P, scol:scol + 2]
        ssq = small[0:P, scol + 2:scol + 3]
        t1 = small[0:P, scol + 3:scol + 4]
        rstd = small[0:P, scol + 4:scol + 5]
        sres = ("epsmall", scol)
        a1 = work[0:P, 0:128]
        a = work[0:P, 128:256]
        sq = work[0:P, 256:384]
        ores = list(ores)
        S.add("dve", lambda e: e.reciprocal(rl[:, 0:1], O0[:, 128:129]), reads=ores, writes=[sres])
        S.add("dve", lambda e: e.reciprocal(rl[:, 1:2], O1[:, 128:129]), reads=ores + [sres], writes=[sres])
        ts("dve", a1, O1[:, 0:128], rl[:, 1:2], sc(LAM, P), ALU.mult, ALU.mult, ores + [sres, "lam"], [wres])
        stt(a, O0[:, 0:128], rl[:, 0:1], a1, ALU.mult, ALU.subtract, ores + [sres, wres], [wres])
        tt("dve", sq, a, a, ALU.mult, [wres], [wres])
        S.add("dve", lambda e: e.reduce_sum(ssq, sq, AX.X), reads=[wres, sres], writes=[sres])
        act(AF.Ln, t1, ssq, [sres, "eps"], [sres], bias=sc(EPSC, P), scale=1.0 / 128.0)
        act(AF.Exp, rstd, t1, [sres], [sres], scale=-0.5)
        stt(attn_out, a, rstd, g08[0:P, :], ALU.mult, ALU.mult, [wres, sres, "g08"], [ares])

    for bh in range(16):
        b, h = bh // 4, bh % 4
        sl = bh % 2
        dma("pool", KTc[sl][:, :], kcT[bh], ("KTc", sl), writes=[("KTc", sl)])
        dma("pool", Vc[sl][:, :, :], vc[bh], ("Vc", sl), writes=[("Vc", sl)])
        ks_ = sl
        km = 2 + sl
        sres, mres = bankres(ks_), bankres(km)
        for t in range(16):
            mm(bank(ks_)[:, t * 32:(t + 1) * 32], KTc[sl][:, t * 128:(t + 1) * 128], Qbd[:, b, h, :],
               True, True, [("KTc", sl), "Qbd"], [sres])
        mm(bank(km)[0:16, 258:290], KTn[:, h, b * 16:(b + 1) * 16], Qbd[:, b, h, :], True, True,
           ["KTn", "Qbd"], [mres])
        act(AF.Exp, Pc[sl][:, :, :], bank(ks_).rearrange("p (t n) -> p t n", n=32), [sres], [("Pc", sl)])
        act(AF.Exp, Pn[sl][:, :], bank(km)[0:16, 258:290], [mres], [("Pn", sl)])
        for c in range(2):
            Oc = bank(km)[0:16, c * 129:(c + 1) * 129]
            for t in range(16):
                mm(Oc[:, 0:128], Pc[sl][:, t, c * 16:(c + 1) * 16], Vc[sl][:, t, :], t == 0, False,
                   [("Pc", sl), ("Vc", sl)], [mres], skip=True)
                mm(Oc[:, 128:129], Pc[sl][:, t, c * 16:(c + 1) * 16], ones_c[:, 0:1], False, False,
                   [("Pc", sl), "ones_c"], [mres], skip=True)
            mm(Oc, Pn[sl][:, c * 16:(c + 1) * 16], Vn[:, b, h, :], False, True,
               [("Pn", sl), "Vn", "Vn1"], [mres], skip=True)
        epilogue(16, bank(km)[0:16, 0:129], bank(km)[0:16, 129:258], [mres], ep_s, "ep_s",
                 attn_s[:, :], "attn_s", 24)
        S.add("pe", lambda e, bh=bh: e.transpose(psT[:, bh * 16:(bh + 1) * 16], attn_s[:, :], ident_b[0:16, 0:16]),
              reads=["attn_s", "identb"], writes=[("ps", 7)])
    for h in range(4):
        cp("dve", mixedT_s[:, 4 + h, :].rearrange("p (b t) -> p b t", t=16),
           psT[:, 0:256].rearrange("p (b h t) -> p h b t", h=4, t=16)[:, h, :, :],
           [("ps", 7)], ["mixedTs"])

    sb.reset(m_A)
    S.barrier()

    if PHASE_LIMIT < 2:
        return emit()
    xTo = [sb.alloc([128, 8, 512], BF16, f"xTo{i}") for i in range(2)]
    xTh = sb.alloc([128, 8, 32], BF16, "xTh")
    C_h = sb.alloc([128, 4, 32], F32, "C_h")
    u_halo = sb.alloc([128, 4, 16, 2], F32, "u_halo")
    gate_g = sb.alloc([128, 4, 512], F32, "gate_g")
    C_g = sb.alloc([128, 4, 512], F32, "C_g")
    u_g = sb.alloc([128, 4, 4, 130], F32, "u_g")
    tconv = sb.alloc([128, 512], F32, "tconv")
    cp_stage = sb.alloc([128, 4, 2], F32, "cp_stage")
    kvst = [sb.alloc([128, 1024], F32, f"kvst{i}") for i in range(2)]

    dma("pool", xTh[:, :, :], xT_halo.rearrange("(k p) n -> p k n", p=128), "xTh", writes=["xTh"])
    k = next_bank()
    for ci in range(8):
        ch = 4 + ci
        for kc in range(8):
            mm(bank(k)[:, ci * 32:(ci + 1) * 32], w_in_t[:, kc, ch * 128:(ch + 1) * 128], xTh[:, kc, :],
               kc == 0, kc == 7, ["w_in", "xTh"], [bankres(k)])
    cp("act", C_h[:, :, :], bank(k)[:, 0:128].rearrange("p (c n) -> p c n", n=32), [bankres(k)], ["C_h"])
    for ch in range(4):
        tt("dve", u_halo[:, ch, :, :], C_h[:, ch, :].rearrange("p (j t) -> p j t", t=2),
           bank(k)[:, 128 + ch * 32:128 + (ch + 1) * 32].rearrange("p (j t) -> p j t", t=2), ALU.mult,
           [bankres(k), "C_h"], ["u_halo"])

    xTo_v = xT_own.rearrange("(k p) n -> p k n", p=128)
    dma("pool", xTo[0][:, :, :], xTo_v[:, :, 0:512], ("xTo", 0), writes=[("xTo", 0)])
    kv_i = 0
    for g in range(4):
        sl = g % 2
        if g + 1 < 4:
            dma("pool", xTo[1 - sl][:, :, :], xTo_v[:, :, (g + 1) * 512:(g + 2) * 512], ("xTo", 1 - sl),
                writes=[("xTo", 1 - sl)])
        xr = ("xTo", sl)
        cp("dve", u_g[:, :, :, 0:2], u_halo[:, :, 4 * g:4 * g + 4, :], ["u_halo"], ["u_g_h"])
        for ch in range(16):
            k = next_bank()
            for kc in range(8):
                mm(bank(k), w_in_t[:, kc, ch * 128:(ch + 1) * 128], xTo[sl][:, kc, :], kc == 0, kc == 7,
                   ["w_in", xr], [bankres(k)])
            if ch < 4:
                cp("act", gate_g[:, ch, :], bank(k), [bankres(k)], [("gate_g", ch)])
            elif ch < 8:
                cp("act", C_g[:, ch - 4, :], bank(k), [bankres(k)], [("C_g", ch - 4)])
            elif ch < 12:
                c4 = ch - 8
                tt("dve", u_g[:, c4, :, 2:130], C_g[:, c4, :].rearrange("p (j t) -> p j t", t=128),
                   bank(k).rearrange("p (j t) -> p j t", t=128), ALU.mult,
                   [bankres(k), ("C_g", c4)], [("u_g", c4)])
                tv = tconv[:, :].rearrange("p (j t) -> p j t", t=128)
                rr = [("u_g", c4), "u_g_h", "convw"]
                ts("dve", tv, u_g[:, c4, :, 0:128], convw_t[:, c4, 0:1], None, ALU.mult, None, rr, ["tconv"])
                stt(tv, u_g[:, c4, :, 1:129], convw_t[:, c4, 1:2], tv, ALU.mult, ALU.add, rr + ["tconv"], ["tconv"])
                stt(tv, u_g[:, c4, :, 2:130], convw_t[:, c4, 2:3], tv, ALU.mult, ALU.add, rr + ["tconv"], ["tconv"])
                tt("dve", mixedT[:, c4, g * 512:(g + 1) * 512], tconv[:, :], gate_g[:, c4, :], ALU.mult,
                   ["tconv", ("gate_g", c4)], [("mixedT", c4, g)])
            else:
                h = ch - 12
                S.add("act", lambda e, h=h, k=k, g=g: e.mul(QT[:, h, g * 512:(g + 1) * 512], bank(k), 0.125),
                      reads=[bankres(k)], writes=[("QT", h, g)])
        if g == 3:
            cp("dve", cp_stage[:, :, :], u_g[:, :, 3, 128:130], [("u_g", c) for c in range(4)], ["cp_stage"])
            dma("sp", conv_p, cp_stage[:, :, :], "convp_o", reads=["cp_stage"])
        for tl in range(4):
            st_i = kv_i % 2
            kv_i += 1
            for half in range(2):
                k = next_bank()
                for kc in range(8):
                    mm(bank(k), xTo[sl][:, kc, tl * 128:(tl + 1) * 128],
                       w_in_t[:, kc, 2048 + half * 512:2048 + (half + 1) * 512], kc == 0, kc == 7,
                       ["w_in", xr], [bankres(k)])
                cp("act" if half == 0 else "dve", kvst[st_i][:, half * 512:(half + 1) * 512], bank(k),
                   [bankres(k)], [("kvst", st_i, half)])
            row = (g * 4 + tl) * 128
            dma("sp", k_own[row:row + 128, :], kvst[st_i][:, 0:512], ("kvst", st_i, 0), reads=[("kvst", st_i, 0)])
            dma("sp", v_own[row:row + 128, :], kvst[st_i][:, 512:1024], ("kvst", st_i, 1), reads=[("kvst", st_i, 1)])

    if PHASE_LIMIT < 3:
        return emit()
    m_A4 = sb.mark()
    xTa = [sb.alloc([128, 8, 512], BF16, f"xTa{i}") for i in range(3)]
    ktst = [sb.alloc([128, 4, 512], BF16, f"ktst{i}") for i in range(2)]
    vst = [sb.alloc([128, 4, 4, 128], BF16, f"vst{i}") for i in range(2)]
    xTa_v = xT_all.rearrange("(k p) n -> p k n", p=128)
    kt_v = kt_s.rearrange("h p n -> p h n")
    v_v = v_s.rearrange("h t n -> t h n")
    NG = 32
    for gi in range(2):
        dma("pool", xTa[gi][:, :, :], xTa_v[:, :, gi * 512:(gi + 1) * 512], ("xTa", gi), writes=[("xTa", gi)])
    for gi in range(NG):
        sl = gi % 3
        if gi + 2 < NG:
            s2 = (gi + 2) % 3
            dma("pool", xTa[s2][:, :, :], xTa_v[:, :, (gi + 2) * 512:(gi + 3) * 512], ("xTa", s2),
                writes=[("xTa", s2)])
        xr = ("xTa", sl)
        st_i = gi % 2
        for h in range(4):
            k = next_bank()
            for kc in range(8):
                mm(bank(k), w_in_t[:, kc, 2048 + h * 128:2048 + (h + 1) * 128], xTa[sl][:, kc, :],
                   kc == 0, kc == 7, ["w_in", xr], [bankres(k)])
            cp("act" if h % 2 == 0 else "dve", ktst[st_i][:, h, :], bank(k), [bankres(k)], [("ktst", st_i)])
        dma("sp", kt_v[:, :, gi * 512:(gi + 1) * 512], ktst[st_i][:, :, :], ("ktst", st_i), reads=[("ktst", st_i)])
        for tl in range(4):
            k = next_bank()
            for kc in range(8):
                mm(bank(k), xTa[sl][:, kc, tl * 128:(tl + 1) * 128], w_in_t[:, kc, 2560:3072],
                   kc == 0, kc == 7, ["w_in", xr], [bankres(k)])
            cp("act" if tl % 2 == 1 else "dve", vst[st_i][:, :, tl, :],
               bank(k).rearrange("p (h d) -> p h d", d=128), [bankres(k)], [("vst", st_i)])
        dma("sp", v_v[:, :, gi * 512:(gi + 1) * 512],
            vst[st_i][:, :, :, :].rearrange("p h t d -> p h (t d)"), ("vst", st_i), reads=[("vst", st_i)])

    sb.reset(m_global)
    S.barrier()

    if PHASE_LIMIT < 4:
        return emit()
    if PHASE_LIMIT < 4:
        return emit()
    off_kt0 = sb.cur
    KT, Vb = [], []
    for i in range(2):
        KT.append(sb.alloc([128, SEQ], BF16, f"KT{i}"))
        Vb.append(sb.alloc([128, NKT, 129], BF16, f"Vb{i}"))
    w_ff1_t = nc.alloc_sbuf_tensor_at("w_ff1_t", [128, 8, 4096], BF16, offset=off_kt0)
    w1v = w_ff1.rearrange("(k p) n -> p k n", p=128)
    Pb = [sb.alloc([128, 2, 512], BF16, f"Pb{i}") for i in range(3)]
    epw = [sb.alloc([128, 384], F32, f"epw{i}") for i in range(2)]
    attn_t = [sb.alloc([128, 128], BF16, f"attn{i}") for i in range(2)]
    Ocp = sb.alloc([128, 8, 129], F32, "Ocp")
    a4 = sb.alloc([128, 4, 128], F32, "a4")
    for i in range(2):
        memset("dve", Vb[i][:, :, 128:129], 1.0, [("Vb1", i)])

    def load_head(h):
        hb = h % 2
        for q in range(4):
            dma("sp", KT[hb][:, q * 4096:(q + 1) * 4096], kt_s[h, :, q * 4096:(q + 1) * 4096],
                ("KT", hb, q), writes=[("KT", hb, q)])
            dma("sp", Vb[hb][:, 32 * q:32 * (q + 1), 0:128],
                v_s[h, :, q * 4096:(q + 1) * 4096].rearrange("p (k d) -> p k d", d=128),
                ("Vb", hb, q), writes=[("Vb", hb, q)])

    def Oacc(idx):
        return psB[idx // 3][:, (idx % 3) * 129:(idx % 3) * 129 + 129]

    deferred = []
    ep_cnt = [0]
    unit_cnt = [0]

    def flush_deferred(u=None):
        keep = []
        for (tu, f) in deferred:
            if u is None or tu == u:
                f()
            else:
                keep.append((tu, f))
        deferred[:] = keep

    load_head(0)
    for h in range(4):
        hb = h % 2
        if h + 1 < 4:
            load_head(h + 1)
        if h == 3:
            for kc2 in range(4):
                dma("pool", w_ff1_t[:, 2 * kc2:2 * kc2 + 2, :], w1v[:, 2 * kc2:2 * kc2 + 2, :], ("w_ff1", kc2),
                    writes=[("w_ff1", kc2), ("Vb1", 0)] + [("KT", 0, q) for q in range(4)]
                    + [("Vb", 0, q) for q in range(4)])
        for G in range(4):
            units = [(jp, rp) for jp in range(4 * G + 4) for rp in range(8)]

            def emit_S(u, jp, rp):
                kidx = jp * 8 + rp
                q = jp // 4
                i0 = max(0, jp - 4 * G)
                c0 = i0 * 128
                sbf = u % 2
                for c in range(2):
                    mm(psA[sbf][c * 64:(c + 1) * 64, c, c0:512] if False else psA[sbf][:, c, c0:512],
                       KT[hb][c * 64:(c + 1) * 64, kidx * 128:(kidx + 1) * 128],
                       QT[c * 64:(c + 1) * 64, h, G * 512 + c0:(G + 1) * 512],
                       True, True, [("KT", hb, q), ("QT", h, G)], [("ps", 2 * sbf + c)])

            def emit_exp(u, jp, rp):
                i0 = max(0, jp - 4 * G)
                c0 = i0 * 128
                sbf, pbf = u % 2, u % 3
                Sx, Px = psA[sbf], Pb[pbf]
                rS = [("ps", 2 * sbf), ("ps", 2 * sbf + 1)]
                if jp < 4 * G:
                    act(AF.Exp, Px[:, :, 0:512], Sx[:, :, 0:512], rS, [("P", pbf, 0)])
                else:
                    if rp == 0:
                        act(AF.Exp, Px[:, :, c0:c0 + 64], Sx[:, :, c0:c0 + 64], rS + ["dmask"],
                            [("P", pbf, 0)], bias=sc(DMASK))
                        act(AF.Exp, Px[:, :, c0 + 64:c0 + 128], Sx[:, :, c0 + 64:c0 + 128], rS,
                            [("P", pbf, 1)])
                    else:
                        act(AF.Exp, Px[:, :, c0:c0 + 128], Sx[:, :, c0:c0 + 128], rS + ["kmask"],
                            [("P", pbf, 0)], bias=sc(KM0 + rp))
                    if i0 < 3:
                        act(AF.Exp, Px[:, :, c0 + 128:512], Sx[:, :, c0 + 128:512], rS, [("P", pbf, 2)])

            def emit_AV(u, jp, rp):
                kidx = jp * 8 + rp
                q = jp // 4
                i0 = max(0, jp - 4 * G)
                pbf = u % 3
                for i in range(i0, 4):
                    for c in range(2):
                        idx = i * 2 + c
                        first = kidx == 0
                        last = kidx == 8 * (4 * G + i) + 7
                        mm(Oacc(idx), Pb[pbf][:, c, i * 128:(i + 1) * 128], Vb[hb][:, kidx, :],
                           first and (idx % 3 == 0), last,
                           [("P", pbf, 0), ("P", pbf, 1), ("P", pbf, 2), ("Vb", hb, q), ("Vb1", hb)],
                           [("ps", 4 + idx // 3)], skip=True)

            nU = len(units)
            emit_S(0, *units[0])
            emit_S(1, *units[1])
            for u in range(nU):
                emit_exp(u, *units[u])
                if u + 2 < nU:
                    emit_S(u + 2, *units[u + 2])
                emit_AV(u, *units[u])
                if deferred:
                    flush_deferred(u)
            for bk in range(3):
                n = 3 if bk < 2 else 2
                cp("dve", Ocp[:, 3 * bk:3 * bk + n, :],
                   psB[bk][:, 0:n * 129].rearrange("p (a d) -> p a d", d=129), [("ps", 4 + bk)], [("Ocp", bk)])

            def stage1(h=h, G=G):
                for i in range(4):
                    O0, O1 = Ocp[:, 2 * i, :], Ocp[:, 2 * i + 1, :]
                    ores = sorted({("Ocp", (2 * i) // 3), ("Ocp", (2 * i + 1) // 3)})
                    rl = small[:, 80 + 2 * i:82 + 2 * i]
                    a1 = epw[0][:, 0:128]
                    sq = epw[0][:, 128:256]
                    a = a4[:, i, :]
                    S.add("dve", lambda e, rl=rl, O0=O0: e.reciprocal(rl[:, 0:1], O0[:, 128:129]), reads=ores, writes=[("rl", i)])
                    S.add("dve", lambda e, rl=rl, O1=O1: e.reciprocal(rl[:, 1:2], O1[:, 128:129]), reads=ores + [("rl", i)], writes=[("rl", i)])
                    ts("dve", a1, O1[:, 0:128], rl[:, 1:2], sc(LAM), ALU.mult, ALU.mult, ores + [("rl", i), "lam"], ["epw_a1"])
                    stt(a, O0[:, 0:128], rl[:, 0:1], a1, ALU.mult, ALU.subtract, ores + [("rl", i), "epw_a1"], [("a4", i)])
                    tt("dve", sq, a, a, ALU.mult, [("a4", i)], ["epw_sq"])
                    S.add("dve", lambda e, i=i, sq=sq: e.reduce_sum(small[:, 96 + i:97 + i], sq, AX.X), reads=["epw_sq"], writes=[("ssq", i)])

            def stage2():
                act(AF.Ln, small[:, 100:104], small[:, 96:100], [("ssq", i) for i in range(4)] + ["eps"], ["ept"],
                    bias=sc(EPSC), scale=1.0 / 128.0)
                act(AF.Exp, small[:, 104:108], small[:, 100:104], ["ept"], ["eprstd"], scale=-0.5)

            def stage3(h=h, G=G):
                for i in range(4):
                    e_i = ep_cnt[0] % 2
                    ep_cnt[0] += 1
                    stt(attn_t[e_i][:, :], a4[:, i, :], small[:, 104 + i:105 + i], g08[:, :], ALU.mult, ALU.mult,
                        [("a4", i), "eprstd", "g08"], [("attn", e_i)])
                    S.add("pe", lambda e, i=i, e_i=e_i: e.transpose(psT[:, i * 128:(i + 1) * 128], attn_t[e_i][:, :], ident_b[:, :]),
                          reads=[("attn", e_i), "identb"], writes=[("ps", 7)])
                    cp("dve", mixedT[:, 4 + h, (4 * G + i) * 128:(4 * G + i + 1) * 128],
                       psT[:, i * 128:(i + 1) * 128], [("ps", 7)], [("mixedT", 4 + h, G, i)])

            deferred.append((1, stage1))
            deferred.append((12, stage2))
            deferred.append((16, stage3))
    flush_deferred()

    w_out_t = nc.alloc_sbuf_tensor_at("w_out_t", [128, 8, D], BF16, offset=m_g2)
    dma("pool", w_out_t[:, :, :], w_out.rearrange("(k p) n -> p k n", p=128), "w_out",
        writes=["w_out"] + [("QT", h, G) for h in range(4) for G in range(4)])

    sb.reset(m_g2)
    S.barrier()

    if PHASE_LIMIT < 5:
        return emit()
    sb.cur = m_g2 + 128 * 8 * 2 * 8
    assert sb.cur <= off_kt0, (sb.cur, off_kt0)
    sb.cur = off_kt0 + 65536
    w2a = sb.alloc_top([128, 16, D], BF16, "w2a")
    w2v = w_ff2.rearrange("(k p) n -> p k n", p=128)
    dma("pool", w2a[:, :, :], w2v[:, 0:16, :], "w2a", writes=["w2a"])
    xt = [sb.alloc([128, D], F32, f"xt{i}") for i in range(2)]
    rt = [sb.alloc([128, D], F32, f"rt{i}") for i in range(2)]
    tmpt0 = sb.alloc([128, D], F32, "tmpt")
    tmpt = [tmpt0, tmpt0]
    ln1_t = sb.alloc([128, 2, D], F32, "ln1")
    dma("sp", ln1_t[:, :, :], lnp[:, 0:2, :], "ln1", writes=["ln1"])
    x1t = [sb.alloc([128, D], F32, f"x1t{i}") for i in range(2)]
    x1Tst = [sb.alloc([128, 8, 128], BF16, f"x1Tst{i}") for i in range(2)]
    x1Tv = x1T_s.rearrange("(k p) n -> p k n", p=128)
    assert sb.cur <= sb.top_cur, (sb.cur, sb.top_cur)

    def c_outproj(tile):
        P = 128 if tile < 16 else 64
        r0 = tile * 128
        sl = tile % 2
        src = x_own[r0:r0 + 128, :] if tile < 16 else xs_d
        dma("sp", xt[sl][0:P, :], src, ("xt", sl), writes=[("xt", sl)])
        pa = psA[sl]
        for half in range(2):
            for mc in range(8):
                lhs = mixedT[:, mc, r0:r0 + 128] if tile < 16 else mixedT_s[:, mc, :]
                rd = ["w_out"]
                if tile < 16:
                    if mc < 4:
                        rd.append(("mixedT", mc, tile // 4))
                    else:
                        rd.append(("mixedT", mc, tile // 4, tile % 4))
                else:
                    rd.append("mixedTs")
                mm(pa[0:P, half, :], lhs, w_out_t[:, mc, half * 512:(half + 1) * 512], mc == 0, mc == 7,
                   rd, [("ps", 2 * sl + half)])

    def c_head(tile):
        P = 128 if tile < 16 else 64
        sl = tile % 2
        pa = psA[sl]
        stt(rt[sl][0:P, :], xt[sl][0:P, :], ALPHA, pa[0:P, :, :].rearrange("p a n -> p (a n)"), ALU.mult, ALU.add,
            [("xt", sl), ("ps", 2 * sl), ("ps", 2 * sl + 1)], [("rt", sl)])
        ln_head(P, rt[sl][0:P, :], ("rt", sl), 40 + 20 * sl)

    def c_post(tile):
        P = 128 if tile < 16 else 64
        r0 = tile * 128
        sl = tile % 2
        pa = psA[sl]
        ln_tail(P, rt[sl][0:P, :], ("rt", sl), x1t[sl][0:P, :], ("x1t", sl), (ln1_t, "ln1"), tmpt[sl][0:P, :],
                ("tmpt", 0), 40 + 20 * sl)
        dma("pool", x1_s[r0:r0 + P, :], x1t[sl][0:P, :], ("x1t", sl), reads=[("x1t", sl)], writes=[("x1s", tile)])
        for kc in range(8):
            pb_ = psB[kc // 4]
            S.add("pe", lambda e, kc=kc, pb_=pb_, P=P, sl=sl: e.transpose(
                pb_[:, (kc % 4) * 128:(kc % 4) * 128 + P], x1t[sl][0:P, kc * 128:(kc + 1) * 128], ident_f[0:P, 0:P]),
                reads=[("x1t", sl), "identf"], writes=[("ps", 4 + kc // 4)])
        for hf in range(2):
            cp("act", x1Tst[sl][:, 4 * hf:4 * hf + 4, 0:P],
               psB[hf][:, :].rearrange("p (k n) -> p k n", n=128)[:, :, 0:P], [("ps", 4 + hf)], [("x1Tst", sl)])
        dma("pool", x1Tv[:, :, r0:r0 + P], x1Tst[sl][:, :, 0:P], ("x1Tst", sl), reads=[("x1Tst", sl)],
            writes=[("x1Ts", tile)])

    c_outproj(0)
    c_outproj(1)
    c_head(0)
    for tile in range(17):
        if tile + 1 < 17:
            c_head(tile + 1)
        if tile + 2 < 17:
            c_outproj(tile + 2)
        c_post(tile)

    sb.reset(m_consts)
    S.barrier()

    if PHASE_LIMIT < 6:
        return emit()
    w2b = sb.alloc([128, 16, D], BF16, "w2b")
    dma("pool", w2b[:, :, :], w2v[:, 16:32, :], "w2b", writes=["w2b"])
    x1Tg0 = sb.alloc([128, 8, 512], BF16, "x1Tg0")
    ln2_t = sb.alloc([128, 2, D], F32, "ln2")
    dma("sp", ln2_t[:, :, :], lnp[:, 2:4, :], "ln2", writes=["ln2"])
    assert sb.cur <= off_kt0, (sb.cur, off_kt0)
    sb.cur = off_kt0 + 65536
    x1Tg = [x1Tg0, sb.alloc([128, 8, 512], BF16, "x1Tg1")]
    hT = sb.alloc([128, 32, 512], BF16, "hT")
    x1g = sb.alloc([128, D], F32, "x1g")
    rt2 = sb.alloc([128, D], F32, "rt2")
    tmp2 = sb.alloc([128, D], F32, "tmp2")
    yst = sb.alloc([128, D], F32, "yst")
    relu_t = sb.alloc([128, 512], F32, "relu")
    assert sb.cur <= sb.top_cur, (sb.cur, sb.top_cur)

    NGRP = 5

    def d_load(g2):
        N2 = 512 if g2 < 4 else 64
        t2 = [4 * g2 + k_ for k_ in range(4)] if g2 < 4 else [16]
        dma("sp", x1Tg[g2 % 2][:, :, 0:N2], x1Tv[:, :, g2 * 512:g2 * 512 + N2], ("x1Tg", g2 % 2),
            reads=[("x1Ts", t) for t in t2], writes=[("x1Tg", g2 % 2)])

    for grp in range(NGRP):
        N = 512 if grp < 4 else 64
        sl = grp % 2
        tiles = [4 * grp + k_ for k_ in range(4)] if grp < 4 else [16]
        for g2 in ([0, 1] if grp == 0 else [grp + 1]):
            if g2 < NGRP:
                d_load(g2)
        for fc in range(32):
            kb = fc % 3
            for kc in range(8):
                mm(psB[kb][:, 0:N], w_ff1_t[:, kc, fc * 128:(fc + 1) * 128], x1Tg[sl][:, kc, 0:N],
                   kc == 0, kc == 7, [("w_ff1", kc // 2), ("x1Tg", sl)], [("ps", 4 + kb)])
            act(AF.Relu, relu_t[:, 0:N], psB[kb][:, 0:N], [("ps", 4 + kb)], ["relu"])
            tt("dve", hT[:, fc, 0:N], relu_t[:, 0:N], relu_t[:, 0:N], ALU.mult, ["relu"], [("hT", fc)])
        for ti, tile in enumerate(tiles):
            P = 128 if tile < 16 else 64
            r0 = tile * 128
            s2 = tile % 2
            dma("sp", x1g[0:P, :], x1_s[r0:r0 + P, :], "x1g", reads=[("x1s", tile)], writes=["x1g"])
            pa = psA[s2]
            for half in range(2):
                for fc in range(32):
                    w2 = w2a if fc < 16 else w2b
                    mm(pa[0:P, half, :], hT[:, fc, ti * 128:ti * 128 + P],
                       w2[:, fc % 16, half * 512:(half + 1) * 512], fc == 0, fc == 31,
                       [("hT", fc), "w2a" if fc < 16 else "w2b"], [("ps", 2 * s2 + half)])
            stt(rt2[0:P, :], x1g[0:P, :], ALPHA, pa[0:P, :, :].rearrange("p a n -> p (a n)"),
                ALU.mult, ALU.add, ["x1g", ("ps", 2 * s2), ("ps", 2 * s2 + 1)], ["rt2"])
            layer_norm(P, rt2[0:P, :], "rt2", yst[0:P, :], "yst", (ln2_t, "ln2"), tmp2[0:P, :], "tmp2", 40)
            dst = y_own[r0:r0 + 128, :] if tile < 16 else ys_o
            dma("pool", dst, yst[0:P, :], "yst", reads=["yst"])

    return emit()


_NC_CACHE = {}


def _get_nc():
    if "nc" not in _NC_CACHE:
        _NC_CACHE["nc"] = build_program()
    return _NC_CACHE["nc"]


def kernel(x_prompt, x_sample, cache_k, cache_v, state_conv, w_in, conv_w,
           lambda_q1, lambda_k1, lambda_q2, lambda_k2, subln_g, w_out,
           ln1_g, ln1_b, w_ff1, w_ff2, ln2_g, ln2_b):
    f32 = np.float32
    X = np.asarray(x_prompt, f32)[0]
    Xt = X.reshape(128, 128, D)
    xs_all = np.asarray(x_sample, f32)
    ck = np.asarray(cache_k, f32)[0]
    cv = np.asarray(cache_v, f32)[0]
    stc = np.asarray(state_conv, f32)[0]
    lnp = np.broadcast_to(np.stack([ln1_g[0], ln1_b[0], ln2_g[0], ln2_b[0]]).astype(f32)[None], (128, 4, D))
    subg = np.broadcast_to(np.asarray(subln_g, f32)[0][None, :], (128, 128))
    lamv = np.broadcast_to(np.stack([lambda_q1[0], lambda_k1[0], lambda_q2[0], lambda_k2[0]]).astype(f32)[None],
                           (128, 4, 64))
    convw = np.asarray(conv_w, f32)[0].T.reshape(4, 128, 3).transpose(1, 0, 2)
    shared = {
        "w_in": np.ascontiguousarray(w_in[0], f32), "w_out": np.ascontiguousarray(w_out[0], f32),
        "w_ff1": np.ascontiguousarray(w_ff1[0], f32), "w_ff2": np.ascontiguousarray(w_ff2[0], f32),
        "lnp": np.ascontiguousarray(lnp), "subg": np.ascontiguousarray(subg),
        "lamv": np.ascontiguousarray(lamv), "convw": np.ascontiguousarray(convw),
        "ident": np.eye(128, dtype=f32),
    }
    in_maps = []
    for c in range(NCORES):
        order = [8 * j + (r + c) % 8 for j in range(16) for r in range(8)]
        own = [8 * j + c for j in range(16)]
        x_own = Xt[own].reshape(2048, D)
        halo = np.zeros((16, 2, D), f32)
        for j in range(16):
            s0 = own[j] * 128
            if s0 >= 2:
                halo[j] = X[s0 - 2:s0]
        xs = xs_all[4 * c:4 * c + 4].reshape(64, D)
        kcT = ck[4 * c:4 * c + 4].transpose(0, 2, 3, 1).reshape(16, 128, 2048)
        vcl = cv[4 * c:4 * c + 4].reshape(4, 16, 128, 4, 128).transpose(0, 3, 2, 1, 4).reshape(16, 128, 16, 128)
        stT = stc[4 * c:4 * c + 4].reshape(4, 2, 4, 128).transpose(3, 2, 0, 1)
        m = dict(shared)
        m.update({
            "xT_all": np.ascontiguousarray(Xt[order].reshape(SEQ, D).T) if PHASE_LIMIT >= 3 else np.ascontiguousarray(X[0:512].T),
            "xT_own": np.ascontiguousarray(x_own.T),
            "x_own": np.ascontiguousarray(x_own),
            "xT_halo": np.ascontiguousarray(halo.reshape(32, D).T),
            "cid": np.full((128, 1), float(c), f32),
            "xsT": np.ascontiguousarray(xs.T), "xs": np.ascontiguousarray(xs),
            "kcT": np.ascontiguousarray(kcT if PHASE_LIMIT >= 1 else kcT[0:1]),
            "vc": np.ascontiguousarray(vcl if PHASE_LIMIT >= 1 else vcl[0:1]),
            "stT": np.ascontiguousarray(stT),
        })
        in_maps.append(m)
    nc = _get_nc()
    res = run_bass_kernel_spmd(nc, in_maps, core_ids=list(range(NCORES)))
    R = res.results
    y_prompt = np.zeros((128, 128, D), f32)
    k_prompt = np.zeros((128, 128, 512), f32)
    v_prompt = np.zeros((128, 128, 512), f32)
    y_sample = np.zeros((32, 16, D), f32)
    k_sample = np.zeros((32, 16, 512), f32)
    v_sample = np.zeros((32, 16, 512), f32)
    conv_sample = np.zeros((32, 2, 512), f32)
    for c in range(NCORES):
        own = [8 * j + c for j in range(16)]
        y_prompt[own] = R[c]["y_own"].reshape(16, 128, D)
        k_prompt[own] = R[c]["k_own"].reshape(16, 128, 512)
        v_prompt[own] = R[c]["v_own"].reshape(16, 128, 512)
        y_sample[4 * c:4 * c + 4] = R[c]["ys"].reshape(4, 16, D)
        k_sample[4 * c:4 * c + 4] = R[c]["ks"].reshape(4, 16, 512)
        v_sample[4 * c:4 * c + 4] = R[c]["vs"].reshape(4, 16, 512)
        conv_sample[4 * c:4 * c + 4] = R[c]["conv_s"].transpose(2, 3, 1, 0).reshape(4, 2, 512)
    conv_prompt = R[7]["conv_p"].transpose(2, 1, 0).reshape(1, 1, 2, 512)
    return (y_prompt.reshape(1, SEQ, D), y_sample,
            k_prompt.reshape(1, 1, SEQ, 4, 128), v_prompt.reshape(1, 1, SEQ, 4, 128),
            np.ascontiguousarray(conv_prompt),
            k_sample.reshape(1, 32, 16, 4, 128), v_sample.reshape(1, 32, 16, 4, 128),
            conv_sample.reshape(1, 32, 2, 512))
```

```python
import os
import numpy as np
import concourse.bass as bass
import concourse.mybir as mybir
from concourse.bass_utils import run_bass_kernel_spmd

F32 = mybir.dt.float32
BF16 = mybir.dt.bfloat16
AF = mybir.ActivationFunctionType
ALU = mybir.AluOpType
AX = mybir.AxisListType

NCORES = 8
SEQ = 16384
D = 1024
NT = 16
NKT = 128
ALPHA = float(2.0 ** 0.25)
EPS = 1e-5
LAMBDA_INIT = 0.2
ONE_M_LI = 1.0 - LAMBDA_INIT
NEG = -30000.0
SEM_LIM = 20000
PHASE_LIMIT = int(os.environ.get('MK_PHASE_LIMIT', '9'))
SUB = int(os.environ.get('MK_SUB', '99'))

ENGS = ("pe", "act", "dve", "pool", "sp")


class _Op:
    __slots__ = ("eng", "fn", "deps", "sig", "dma_key", "sem", "val", "waits")


class Sched:
    def __init__(self):
        self.ops = {e: [] for e in ENGS}
        self.lastw = {}
        self.readers = {}
        self.last_op = {}
        self.dmas = []
        self.dma_queue = {}
        self.ps_acc = {}

    def add(self, eng, fn, reads=(), writes=(), dma_key=None, extra_deps=()):
        o = _Op()
        o.eng = eng
        o.fn = fn
        o.sig = False
        o.dma_key = dma_key
        deps = {}
        ps_r = [r for r in reads if isinstance(r, tuple) and r[0] == "ps"]
        ps_w = [r for r in writes if isinstance(r, tuple) and r[0] == "ps"]
        reads = [r for r in reads if not (isinstance(r, tuple) and r[0] == "ps")]
        writes = [r for r in writes if not (isinstance(r, tuple) and r[0] == "ps")]
        for r, isw in [(x, False) for x in ps_r] + [(x, True) for x in ps_w]:
            acc = self.ps_acc.setdefault(r, {})
            for e2, (o2, w2) in acc.items():
                if e2 != eng or isw or w2:
                    deps[id(o2)] = o2
        for r in reads:
            w = self.lastw.get(r)
            if w is not None:
                deps[id(w)] = w
        for r in writes:
            w = self.lastw.get(r)
            if w is not None:
                deps[id(w)] = w
            for rd in self.readers.get(r, ()):
                deps[id(rd)] = rd
        for d in extra_deps:
            deps[id(d)] = d
        o.deps = [d for d in deps.values()
                  if not (eng == "pe" and d.eng == "pe" and d.dma_key is None and dma_key is None)]
        for r in reads:
            lst = self.readers.setdefault(r, [])
            if dma_key is None:
                lst[:] = [x for x in lst if not (x.eng == eng and x.dma_key is None)]
            lst.append(o)
        for r in writes:
            self.lastw[r] = o
            self.readers[r] = []
        for r, isw in [(x, False) for x in ps_r] + [(x, True) for x in ps_w]:
            prev = self.ps_acc[r].get(eng)
            self.ps_acc[r][eng] = (o, isw or (prev is not None and prev[1] and prev[0] is o))
        for d in o.deps:
            d.sig = True
        self.ops[eng].append(o)
        if dma_key is None:
            if fn is not None:
                self.last_op[eng] = o
        else:
            o.sig = True
            self.dmas.append(o)
            q = self.dma_queue.setdefault(dma_key, eng)
            assert q == eng, (dma_key, q, eng)
        return o

    def barrier(self):
        deps = list(self.last_op.values()) + list(self.dmas)
        self.dmas = []
        for e in ENGS:
            self.add(e, None, extra_deps=deps)

    def finalize(self):
        sem_names = []
        dma_cnt = {}
        for e in ENGS:
            cnt = 0
            for o in self.ops[e]:
                if o.dma_key is not None:
                    k = ("dma", o.dma_key)
                    dma_cnt[k] = dma_cnt.get(k, 0) + 16
                    o.sem = k
                    o.val = dma_cnt[k]
                    if k not in sem_names:
                        sem_names.append(k)
                elif o.sig:
                    ep = cnt // SEM_LIM
                    o.sem = ("eng", e, ep)
                    o.val = cnt % SEM_LIM + 1
                    cnt += 1
                    if o.sem not in sem_names:
                        sem_names.append(o.sem)
        for e in ENGS:
            for o in self.ops[e]:
                w = {}
                for d in o.deps:
                    if w.get(d.sem, 0) < d.val:
                        w[d.sem] = d.val
                o.waits = w
        self.final = dict(dma_cnt)
        return sem_names

    def emit_engine(self, eng_name, e, sems):
        seen = {}
        for o in self.ops[eng_name]:
            for k, v in o.waits.items():
                if seen.get(k, 0) < v:
                    e.wait_ge(sems[k], v)
                    seen[k] = v
            if o.fn is None:
                continue
            inst = o.fn(e)
            if o.sig:
                inst.then_inc(sems[o.sem], 16 if o.dma_key is not None else 1)
        if eng_name == "sp":
            for k, v in self.final.items():
                if seen.get(k, 0) < v:
                    e.wait_ge(sems[k], v)


class SBAlloc:
    def __init__(self, nc):
        self.nc = nc
        self.base = 16512
        self.top = 229344
        self.cur = self.base
        self.top_cur = self.top
        self.n = 0

    def alloc(self, shape, dt, name="t"):
        sz = 1
        for s in shape[1:]:
            sz *= s
        sz *= 4 if dt == F32 else 2
        sz = (sz + 31) // 32 * 32
        off = self.cur
        self.cur += sz
        assert self.cur <= self.top, ("SBUF overflow", name, self.cur - self.base)
        self.n += 1
        return self.nc.alloc_sbuf_tensor_at(f"{name}_{self.n}", list(shape), dt, offset=off)

    def alloc_top(self, shape, dt, name="t"):
        sz = 1
        for s in shape[1:]:
            sz *= s
        sz *= 4 if dt == F32 else 2
        sz = (sz + 31) // 32 * 32
        self.top_cur -= sz
        self.n += 1
        return self.nc.alloc_sbuf_tensor_at(f"{name}_{self.n}", list(shape), dt, offset=self.top_cur)

    def mark(self):
        return self.cur

    def reset(self, m):
        self.cur = m


def build_program():
    nc = bass.Bass("TRN2", target_bir_lowering=False)
    S = Sched()
    sb = SBAlloc(nc)

    def emit():
        sem_names = S.finalize()
        sems = {k: nc.alloc_semaphore(name=f"s{i}") for i, k in enumerate(sem_names)}
        with nc.Block() as block:
            @block.tensor
            def _(e):
                S.emit_engine("pe", e, sems)

            @block.scalar
            def _(e):
                S.emit_engine("act", e, sems)

            @block.vector
            def _(e):
                S.emit_engine("dve", e, sems)

            @block.gpsimd
            def _(e):
                S.emit_engine("pool", e, sems)

            @block.sync
            def _(e):
                S.emit_engine("sp", e, sems)
        return nc

    def din(name, shape, dt=F32):
        return nc.dram_tensor(name, list(shape), dt, kind="ExternalInput").ap()

    def dout(name, shape, dt=F32):
        return nc.dram_tensor(name, list(shape), dt, kind="ExternalOutput").ap()

    def dint(name, shape, dt):
        return nc.dram_tensor(name, list(shape), dt, kind="Internal").ap()

    xT_all = din("xT_all", [D, SEQ if PHASE_LIMIT >= 3 else 512])
    xT_own = din("xT_own", [D, 2048])
    x_own = din("x_own", [2048, D])
    xT_halo = din("xT_halo", [D, 32])
    w_in = din("w_in", [D, 3072])
    w_out = din("w_out", [D, D])
    w_ff1 = din("w_ff1", [D, 4096])
    w_ff2 = din("w_ff2", [4096, D])
    lnp = din("lnp", [128, 4, D])
    subg = din("subg", [128, 128])
    lamv = din("lamv", [128, 4, 64])
    convw = din("convw", [128, 4, 3])
    cid = din("cid", [128, 1])
    ident_d = din("ident", [128, 128])
    xsT = din("xsT", [D, 64])
    xs_d = din("xs", [64, D])
    kcT = din("kcT", [16 if PHASE_LIMIT >= 1 else 1, 128, 2048])
    vc = din("vc", [16 if PHASE_LIMIT >= 1 else 1, 128, 16, 128])
    stT = din("stT", [128, 4, 4, 2])

    y_own = dout("y_own", [2048, D])
    ys_o = dout("ys", [64, D])
    k_own = dout("k_own", [2048, 512])
    v_own = dout("v_own", [2048, 512])
    conv_p = dout("conv_p", [128, 4, 2])
    ks_o = dout("ks", [64, 512])
    vs_o = dout("vs", [64, 512])
    conv_s = dout("conv_s", [128, 4, 4, 2])

    kt_s = dint("kt_s", [4, 128, SEQ], BF16)
    v_s = dint("v_s", [4, 128, NKT * 128], BF16)
    x1_s = dint("x1_s", [2048 + 64, D], F32)
    x1T_s = dint("x1T_s", [D, 2048 + 64], BF16)

    psA = [nc.alloc_psum_tensor(f"psA{i}", [128, 2, 512], F32) for i in range(2)]
    psB = [nc.alloc_psum_tensor(f"psB{i}", [128, 512], F32) for i in range(3)]
    psT = nc.alloc_psum_tensor("psT", [128, 1024], BF16)

    def bank(k):
        return psA[k // 2][:, k % 2, :] if k < 4 else psB[k - 4][:, :]

    def bankres(k):
        return ("ps", k)

    bank_rr = [0]

    def next_bank():
        k = bank_rr[0] % 7
        bank_rr[0] += 1
        return k

    def dma(q, out, in_, key, reads=(), writes=()):
        return S.add(q, lambda e, out=out, in_=in_: e.dma_start(out=out, in_=in_),
                     reads=reads, writes=writes, dma_key=key)

    def mm(out, lhsT, rhs, start, stop, reads, writes, skip=False):
        if skip:
            fn = lambda e: e.matmul(out, lhsT, rhs, start=start, stop=stop, skip_group_check=True)
        else:
            fn = lambda e: e.matmul(out, lhsT, rhs, start=start, stop=stop)
        return S.add("pe", fn, reads=reads, writes=writes)

    def act(func, out, in_, reads, writes, bias=None, scale=None):
        kw = {}
        if bias is not None:
            kw["bias"] = bias
        if scale is not None:
            kw["scale"] = scale
        return S.add("act", lambda e: e.activation(out, in_, func, **kw), reads=reads, writes=writes)

    def ts(eng, out, in0, s1, s2, op0, op1, reads, writes):
        if op1 is None:
            fn = lambda e: e.tensor_scalar(out, in0, s1, None, op0)
        else:
            fn = lambda e: e.tensor_scalar(out, in0, s1, s2, op0, op1)
        return S.add(eng, fn, reads=reads, writes=writes)

    def tt(eng, out, in0, in1, op, reads, writes):
        return S.add(eng, lambda e: e.tensor_tensor(out, in0, in1, op), reads=reads, writes=writes)

    def stt(out, in0, scalar, in1, op0, op1, reads, writes):
        return S.add("dve", lambda e: e.scalar_tensor_tensor(out, in0, scalar, in1, op0, op1),
                     reads=reads, writes=writes)

    def cp(eng, out, in_, reads, writes):
        if eng == "act":
            return S.add("act", lambda e: e.activation(out, in_, AF.Copy), reads=reads, writes=writes)
        return S.add(eng, lambda e: e.tensor_copy(out, in_), reads=reads, writes=writes)

    def memset(eng, ap, val, writes):
        return S.add(eng, lambda e: e.memset(ap, val), writes=writes)

    ident_f = sb.alloc([128, 128], F32, "identf")
    ident_b = sb.alloc([128, 128], BF16, "identb")
    g08 = sb.alloc([128, 128], F32, "g08")
    lamv_t = sb.alloc([128, 4, 64], F32, "lamv")
    convw_t = sb.alloc([128, 4, 3], F32, "convw")
    small = sb.alloc([128, 128], F32, "small")
    junk = sb.alloc([128, 64], F32, "junk")
    m_consts = sb.mark()
    mixedT = sb.alloc([128, 8, 2048], BF16, "mixedT")
    mixedT_s = sb.alloc([128, 8, 64], BF16, "mixedTs")
    m_g2 = sb.mark()
    QT = sb.alloc([128, 4, 2048], BF16, "QT")
    EPSC, DMASK, LAM, CIDC, E1, E2, S1, S2 = 0, 1, 2, 3, 4, 5, 6, 7
    RV0 = 8
    KM0 = 16

    def sc(c, p=128):
        return small[0:p, c:c + 1]

    dma("sp", ident_f[:, :], ident_d, "identf", writes=["identf"])
    dma("sp", g08[:, :], subg, "g08", writes=["g08"])
    dma("sp", lamv_t[:, :, :], lamv, "lamv", writes=["lamv"])
    dma("sp", convw_t[:, :, :], convw, "convw", writes=["convw"])
    dma("sp", sc(CIDC), cid, "cid", writes=["cid"])
    cp("dve", ident_b[:, :], ident_f[:, :], ["identf"], ["identb"])
    ts("dve", g08[:, :], g08[:, :], ONE_M_LI, None, ALU.mult, None, ["g08"], ["g08"])
    memset("dve", sc(EPSC), EPS, ["eps"])
    memset("dve", small[0:64, DMASK:DMASK + 1], 0.0, ["dmask"])
    memset("dve", small[64:128, DMASK:DMASK + 1], NEG, ["dmask"])
    for r in range(8):
        memset("dve", sc(RV0 + r), float(r), ["rv"])
    ts("dve", small[:, KM0:KM0 + 8], small[:, RV0:RV0 + 8], sc(CIDC), 8.0, ALU.add, ALU.is_ge,
       ["rv", "cid"], ["kmask"])
    ts("dve", small[:, KM0:KM0 + 8], small[:, KM0:KM0 + 8], -1.0, -NEG, ALU.add, ALU.mult,
       ["kmask"], ["kmask"])
    tt("dve", junk[:, 0:64], lamv_t[:, 0, :], lamv_t[:, 1, :], ALU.mult, ["lamv"], ["junk"])
    S.add("dve", lambda e: e.reduce_sum(sc(S1), junk[:, 0:64], AX.X), reads=["junk"], writes=["s1"])
    tt("dve", junk[:, 0:64], lamv_t[:, 2, :], lamv_t[:, 3, :], ALU.mult, ["lamv"], ["junk"])
    S.add("dve", lambda e: e.reduce_sum(sc(S2), junk[:, 0:64], AX.X), reads=["junk"], writes=["s2"])
    act(AF.Exp, sc(E1), sc(S1), ["s1"], ["e1"])
    act(AF.Exp, sc(E2), sc(S2), ["s2"], ["e2"])
    tt("dve", sc(LAM), sc(E1), sc(E2), ALU.subtract, ["e1", "e2"], ["lam"])
    ts("dve", sc(LAM), sc(LAM), LAMBDA_INIT, None, ALU.add, None, ["lam"], ["lam"])

    m_global = sb.mark()
    if PHASE_LIMIT < -1:
        return emit()

    def ln_head(P, r, rres, scol):
        st = small[0:P, scol:scol + 12]
        mv = small[0:P, scol + 12:scol + 14]
        t1 = small[0:P, scol + 14:scol + 15]
        rstd = small[0:P, scol + 15:scol + 16]
        sres = ("lnsmall", scol)
        S.add("dve", lambda e: e.bn_stats(st[:, 0:6], r[:, 0:512]), reads=[rres], writes=[sres])
        S.add("dve", lambda e: e.bn_stats(st[:, 6:12], r[:, 512:1024]), reads=[rres, sres], writes=[sres])
        S.add("dve", lambda e: e.bn_aggr(mv, st), reads=[sres], writes=[sres])
        act(AF.Ln, t1, mv[:, 1:2], [sres, "eps"], [("lnt1", scol)], bias=sc(EPSC, P))
        act(AF.Exp, rstd, t1, [("lnt1", scol)], [("lnrstd", scol)], scale=-0.5)
        m1 = small[0:P, scol + 17:scol + 18]
        nb = small[0:P, scol + 16:scol + 17]
        act(AF.Copy, m1, mv[:, 0:1], [sres, ("lnrstd", scol)], [("lnm1", scol)], scale=rstd)
        S.add("act", lambda e: e.mul(nb, m1, -1.0), reads=[("lnm1", scol)], writes=[("lnnb", scol)])

    def ln_tail(P, r, rres, out, outres, gt, tmp, tmpres, scol, part=0):
        rstd = small[0:P, scol + 15:scol + 16]
        nb = small[0:P, scol + 16:scol + 17]
        if part in (0, 1):
            act(AF.Identity, tmp, r, [rres, ("lnrstd", scol), ("lnnb", scol)], [tmpres], bias=nb, scale=rstd)
        if part in (0, 2):
            tt("dve", tmp, tmp, gt[0][0:P, 0, :], ALU.mult, [tmpres, gt[1]], [tmpres])
            tt("dve", out, tmp, gt[0][0:P, 1, :], ALU.add, [tmpres, gt[1]], [outres])

    def layer_norm(P, r, rres, out, outres, gt, tmp, tmpres, scol):
        ln_head(P, r, rres, scol)
        ln_tail(P, r, rres, out, outres, gt, tmp, tmpres, scol)

    w_in_t = sb.alloc([128, 8, 3072], BF16, "w_in")
    dma("pool", w_in_t[:, :, :], w_in.rearrange("(k p) n -> p k n", p=128), "w_in", writes=["w_in"])
    m_A = sb.mark()

    if PHASE_LIMIT < 0:
        return emit()
    xsT_t = sb.alloc([128, 8, 64], BF16, "xsT")
    gate_s = sb.alloc([128, 4, 64], F32, "gate_s")
    C_s = sb.alloc([128, 4, 64], F32, "C_s")
    uext_s = sb.alloc([128, 4, 4, 18], F32, "uext_s")
    tconv_s = sb.alloc([128, 4, 64], F32, "tconv_s")
    Qbd = sb.alloc([128, 4, 4, 32], BF16, "Qbd")
    KTn = sb.alloc([128, 4, 64], BF16, "KTn")
    Vn = sb.alloc([16, 4, 4, 129], BF16, "Vn")
    ksst = sb.alloc([64, 512], F32, "ksst")
    vsst = sb.alloc([16, 4, 512], F32, "vsst")

    dma("pool", xsT_t[:, :, :], xsT.rearrange("(k p) n -> p k n", p=128), "xsT", writes=["xsT"])
    st_stage = sb.alloc([128, 4, 4, 2], F32, "st_stage")
    cs_stage = sb.alloc([128, 4, 4, 2], F32, "cs_stage")
    dma("sp", st_stage[:, :, :, :], stT, "st_stage", writes=["st_stage"])
    cp("dve", uext_s[:, :, :, 0:2], st_stage[:, :, :, :], ["st_stage"], ["uext_s_st"])
    memset("dve", Qbd[:, :, :, :], 0.0, ["Qbd"])
    memset("dve", Vn[:, :, :, 128:129], 1.0, ["Vn1"])

    if SUB < -2:
        return emit()
    for grp in range(3):
        if (SUB == -2 and grp == 1) or (SUB == -1 and grp == 2):
            return emit()
        chunks = list(range(grp * 8, min(grp * 8 + 8, 20)))
        k = next_bank()
        for ci, ch in enumerate(chunks):
            for kc in range(8):
                mm(bank(k)[:, ci * 64:(ci + 1) * 64], w_in_t[:, kc, ch * 128:(ch + 1) * 128],
                   xsT_t[:, kc, :], kc == 0, kc == 7, ["w_in", "xsT"], [bankres(k)])
        bk = bank(k)
        if grp == 0:
            cp("act", gate_s[:, :, :], bk[:, 0:256].rearrange("p (c n) -> p c n", n=64),
               [bankres(k)], ["gate_s"])
            cp("act", C_s[:, :, :], bk[:, 256:512].rearrange("p (c n) -> p c n", n=64),
               [bankres(k)], ["C_s"])
        elif grp == 1:
            for ch in range(4):
                tt("dve", uext_s[:, ch, :, 2:18],
                   C_s[:, ch, :].rearrange("p (b t) -> p b t", t=16),
                   bk[:, ch * 64:(ch + 1) * 64].rearrange("p (b t) -> p b t", t=16), ALU.mult,
                   [bankres(k), "C_s"], ["uext_s"])
            for h in range(4):
                src = bk[:, 256 + h * 64:256 + (h + 1) * 64].rearrange("p (b t) -> p b t", t=16)
                S.add("act", lambda e, h=h, src=src: e.mul(Qbd[0:64, :, h, 0:16], src[0:64], 0.125),
                      reads=[bankres(k)], writes=["Qbd"])
                S.add("act", lambda e, h=h, src=src: e.mul(Qbd[64:128, :, h, 16:32], src[64:128], 0.125),
                      reads=[bankres(k)], writes=["Qbd"])
        else:
            cp("dve", KTn[:, :, :], bk[:, 0:256].rearrange("p (h n) -> p h n", n=64),
               [bankres(k)], ["KTn"])
    if SUB < 1:
        return emit()
    k = next_bank()
    for kc in range(8):
        mm(bank(k)[0:64, :], xsT_t[:, kc, :], w_in_t[:, kc, 2048:2560], kc == 0, kc == 7,
           ["w_in", "xsT"], [bankres(k)])
    cp("dve", ksst[:, :], bank(k)[0:64, :], [bankres(k)], ["ksst"])
    dma("sp", ks_o, ksst[:, :], "ksst", reads=["ksst"])
    if SUB < 2:
        return emit()
    for b in range(4):
        k = next_bank()
        for kc in range(8):
            mm(bank(k)[0:16, :], xsT_t[:, kc, b * 16:(b + 1) * 16], w_in_t[:, kc, 2560:3072],
               kc == 0, kc == 7, ["w_in", "xsT"], [bankres(k)])
        cp("dve", vsst[:, b, :], bank(k)[0:16, :], [bankres(k)], ["vsst"])
        cp("act", Vn[:, b, :, 0:128], bank(k)[0:16, :].rearrange("p (h d) -> p h d", d=128),
           [bankres(k)], ["Vn"])
    dma("sp", vs_o.rearrange("(b t) c -> t b c", t=16), vsst[:, :, :], "vsst", reads=["vsst"])
    if SUB < 3:
        return emit()
    cp("dve", cs_stage[:, :, :, :], uext_s[:, :, :, 16:18], ["uext_s", "uext_s_st"], ["cs_stage"])
    dma("sp", conv_s, cs_stage[:, :, :, :], "convs_o", reads=["cs_stage"])
    for ch in range(4):
        tv = tconv_s[:, ch, :].rearrange("p (b t) -> p b t", t=16)
        rr = ["uext_s", "uext_s_st", "convw"]
        ts("dve", tv, uext_s[:, ch, :, 0:16], convw_t[:, ch, 0:1], None, ALU.mult, None, rr, ["tconv_s"])
        stt(tv, uext_s[:, ch, :, 1:17], convw_t[:, ch, 1:2], tv, ALU.mult, ALU.add, rr + ["tconv_s"], ["tconv_s"])
        stt(tv, uext_s[:, ch, :, 2:18], convw_t[:, ch, 2:3], tv, ALU.mult, ALU.add, rr + ["tconv_s"], ["tconv_s"])
        tt("dve", mixedT_s[:, ch, :], tconv_s[:, ch, :], gate_s[:, ch, :], ALU.mult,
           ["tconv_s", "gate_s"], ["mixedTs"])

    if PHASE_LIMIT < 1:
        return emit()
    KTc = [sb.alloc([128, 2048], BF16, f"KTc{i}") for i in range(2)]
    Vc = [sb.alloc([128, 16, 128], BF16, f"Vc{i}") for i in range(2)]
    ones_c = sb.alloc([128, 2], BF16, "ones_c")
    memset("dve", ones_c[:, :], 1.0, ["ones_c"])
    Pc = [sb.alloc([128, 16, 32], BF16, f"Pc{i}") for i in range(2)]
    Pn = [sb.alloc([16, 32], BF16, f"Pn{i}") for i in range(2)]
    ep_s = sb.alloc([16, 512], F32, "ep_s")
    attn_s = sb.alloc([16, 128], BF16, "attn_s")

    def epilogue(P, O0, O1, ores, work, wres, attn_out, ares, scol):
        rl = small[0:P, scol:scol + 2]
        ssq = small[0:P, scol + 2:scol + 3]
        t1 = small[0:P, scol + 3:scol + 4]
        rstd = small[0:P, scol + 4:scol + 5]
        sres = ("epsmall", scol)
        a1 = work[0:P, 0:128]
        a = work[0:P, 128:256]
        sq = work[0:P, 256:384]
        ores = list(ores)
        S.add("dve", lambda e: e.reciprocal(rl[:, 0:1], O0[:, 128:129]), reads=ores, writes=[sres])
        S.add("dve", lambda e: e.reciprocal(rl[:, 1:2], O1[:, 128:129]), reads=ores + [sres], writes=[sres])
        ts("dve", a1, O1[:, 0:128], rl[:, 1:2], sc(LAM, P), ALU.mult, ALU.mult, ores + [sres, "lam"], [wres])
        stt(a, O0[:, 0:128], rl[:, 0:1], a1, ALU.mult, ALU.subtract, ores + [sres, wres], [wres])
        tt("dve", sq, a, a, ALU.mult, [wres], [wres])
        S.add("dve", lambda e: e.reduce_sum(ssq, sq, AX.X), reads=[wres, sres], writes=[sres])
        act(AF.Ln, t1, ssq, [sres, "eps"], [sres], bias=sc(EPSC, P), scale=1.0 / 128.0)
        act(AF.Exp, rstd, t1, [sres], [sres], scale=-0.5)
        stt(attn_out, a, rstd, g08[0:P, :], ALU.mult, ALU.mult, [wres, sres, "g08"], [ares])

    for bh in range(16):
        b, h = bh // 4, bh % 4
        sl = bh % 2
        dma("pool", KTc[sl][:, :], kcT[bh], ("KTc", sl), writes=[("KTc", sl)])
        dma("pool", Vc[sl][:, :, :], vc[bh], ("Vc", sl), writes=[("Vc", sl)])
        ks_ = sl
        km = 2 + sl
        sres, mres = bankres(ks_), bankres(km)
        for t in range(16):
            mm(bank(ks_)[:, t * 32:(t + 1) * 32], KTc[sl][:, t * 128:(t + 1) * 128], Qbd[:, b, h, :],
               True, True, [("KTc", sl), "Qbd"], [sres])
        mm(bank(km)[0:16, 258:290], KTn[:, h, b * 16:(b + 1) * 16], Qbd[:, b, h, :], True, True,
           ["KTn", "Qbd"], [mres])
        act(AF.Exp, Pc[sl][:, :, :], bank(ks_).rearrange("p (t n) -> p t n", n=32), [sres], [("Pc", sl)])
        act(AF.Exp, Pn[sl][:, :], bank(km)[0:16, 258:290], [mres], [("Pn", sl)])
        for c in range(2):
            Oc = bank(km)[0:16, c * 129:(c + 1) * 129]
            for t in range(16):
                mm(Oc[:, 0:128], Pc[sl][:, t, c * 16:(c + 1) * 16], Vc[sl][:, t, :], t == 0, False,
                   [("Pc", sl), ("Vc", sl)], [mres], skip=True)
                mm(Oc[:, 128:129], Pc[sl][:, t, c * 16:(c + 1) * 16], ones_c[:, 0:1], False, False,
                   [("Pc", sl), "ones_c"], [mres], skip=True)
            mm(Oc, Pn[sl][:, c * 16:(c + 1) * 16], Vn[:, b, h, :], False, True,
               [("Pn", sl), "Vn", "Vn1"], [mres], skip=True)
        epilogue(16, bank(km)[0:16, 0:129], bank(km)[0:16, 129:258], [mres], ep_s, "ep_s",
                 attn_s[:, :], "attn_s", 24)
        S.add("pe", lambda e, bh=bh: e.transpose(psT[:, bh * 16:(bh + 1) * 16], attn_s[:, :], ident_b[0:16, 0:16]),
              reads=["attn_s", "identb"], writes=[("ps", 7)])
    for h in range(4):
        cp("dve", mixedT_s[:, 4 + h, :].rearrange("p (b t) -> p b t", t=16),
           psT[:, 0:256].rearrange("p (b h t) -> p h b t", h=4, t=16)[:, h, :, :],
           [("ps", 7)], ["mixedTs"])

    sb.reset(m_A)
    S.barrier()

    if PHASE_LIMIT < 2:
        return emit()
    xTo = [sb.alloc([128, 8, 512], BF16, f"xTo{i}") for i in range(2)]
    xTh = sb.alloc([128, 8, 32], BF16, "xTh")
    C_h = sb.alloc([128, 4, 32], F32, "C_h")
    u_halo = sb.alloc([128, 4, 16, 2], F32, "u_halo")
    gate_g = sb.alloc([128, 4, 512], F32, "gate_g")
    C_g = sb.alloc([128, 4, 512], F32, "C_g")
    u_g = sb.alloc([128, 4, 4, 130], F32, "u_g")
    tconv = sb.alloc([128, 512], F32, "tconv")
    cp_stage = sb.alloc([128, 4, 2], F32, "cp_stage")
    kvst = [sb.alloc([128, 1024], F32, f"kvst{i}") for i in range(2)]

    dma("pool", xTh[:, :, :], xT_halo.rearrange("(k p) n -> p k n", p=128), "xTh", writes=["xTh"])
    k = next_bank()
    for ci in range(8):
        ch = 4 + ci
        for kc in range(8):
            mm(bank(k)[:, ci * 32:(ci + 1) * 32], w_in_t[:, kc, ch * 128:(ch + 1) * 128], xTh[:, kc, :],
               kc == 0, kc == 7, ["w_in", "xTh"], [bankres(k)])
    cp("act", C_h[:, :, :], bank(k)[:, 0:128].rearrange("p (c n) -> p c n", n=32), [bankres(k)], ["C_h"])
    for ch in range(4):
        tt("dve", u_halo[:, ch, :, :], C_h[:, ch, :].rearrange("p (j t) -> p j t", t=2),
           bank(k)[:, 128 + ch * 32:128 + (ch + 1) * 32].rearrange("p (j t) -> p j t", t=2), ALU.mult,
           [bankres(k), "C_h"], ["u_halo"])

    xTo_v = xT_own.rearrange("(k p) n -> p k n", p=128)
    dma("pool", xTo[0][:, :, :], xTo_v[:, :, 0:512], ("xTo", 0), writes=[("xTo", 0)])
    kv_i = 0
    for g in range(4):
        sl = g % 2
        if g + 1 < 4:
            dma("pool", xTo[1 - sl][:, :, :], xTo_v[:, :, (g + 1) * 512:(g + 2) * 512], ("xTo", 1 - sl),
                writes=[("xTo", 1 - sl)])
        xr = ("xTo", sl)
        cp("dve", u_g[:, :, :, 0:2], u_halo[:, :, 4 * g:4 * g + 4, :], ["u_halo"], ["u_g_h"])
        for ch in range(16):
            k = next_bank()
            for kc in range(8):
                mm(bank(k), w_in_t[:, kc, ch * 128:(ch + 1) * 128], xTo[sl][:, kc, :], kc == 0, kc == 7,
                   ["w_in", xr], [bankres(k)])
            if ch < 4:
                cp("act", gate_g[:, ch, :], bank(k), [bankres(k)], [("gate_g", ch)])
            elif ch < 8:
                cp("act", C_g[:, ch - 4, :], bank(k), [bankres(k)], [("C_g", ch - 4)])
            elif ch < 12:
                c4 = ch - 8
                tt("dve", u_g[:, c4, :, 2:130], C_g[:, c4, :].rearrange("p (j t) -> p j t", t=128),
                   bank(k).rearrange("p (j t) -> p j t", t=128), ALU.mult,
                   [bankres(k), ("C_g", c4)], [("u_g", c4)])
                tv = tconv[:, :].rearrange("p (j t) -> p j t", t=128)
                rr = [("u_g", c4), "u_g_h", "convw"]
                ts("dve", tv, u_g[:, c4, :, 0:128], convw_t[:, c4, 0:1], None, ALU.mult, None, rr, ["tconv"])
                stt(tv, u_g[:, c4, :, 1:129], convw_t[:, c4, 1:2], tv, ALU.mult, ALU.add, rr + ["tconv"], ["tconv"])
                stt(tv, u_g[:, c4, :, 2:130], convw_t[:, c4, 2:3], tv, ALU.mult, ALU.add, rr + ["tconv"], ["tconv"])
                tt("dve", mixedT[:, c4, g * 512:(g + 1) * 512], tconv[:, :], gate_g[:, c4, :], ALU.mult,
                   ["tconv", ("gate_g", c4)], [("mixedT", c4, g)])
            else:
                h = ch - 12
                S.add("act", lambda e, h=h, k=k, g=g: e.mul(QT[:, h, g * 512:(g + 1) * 512], bank(k), 0.125),
                      reads=[bankres(k)], writes=[("QT", h, g)])
        if g == 3:
            cp("dve", cp_stage[:, :, :], u_g[:, :, 3, 128:130], [("u_g", c) for c in range(4)], ["cp_stage"])
            dma("sp", conv_p, cp_stage[:, :, :], "convp_o", reads=["cp_stage"])
        for tl in range(4):
            st_i = kv_i % 2
            kv_i += 1
            for half in range(2):
                k = next_bank()
                for kc in range(8):
                    mm(bank(k), xTo[sl][:, kc, tl * 128:(tl + 1) * 128],
                       w_in_t[:, kc, 2048 + half * 512:2048 + (half + 1) * 512], kc == 0, kc == 7,
                       ["w_in", xr], [bankres(k)])
                cp("act" if half == 0 else "dve", kvst[st_i][:, half * 512:(half + 1) * 512], bank(k),
                   [bankres(k)], [("kvst", st_i, half)])
            row = (g * 4 + tl) * 128
            dma("sp", k_own[row:row + 128, :], kvst[st_i][:, 0:512], ("kvst", st_i, 0), reads=[("kvst", st_i, 0)])
            dma("sp", v_own[row:row + 128, :], kvst[st_i][:, 512:1024], ("kvst", st_i, 1), reads=[("kvst", st_i, 1)])

    if PHASE_LIMIT < 3:
        return emit()
    m_A4 = sb.mark()
    xTa = [sb.alloc([128, 8, 512], BF16, f"xTa{i}") for i in range(3)]
    ktst = [sb.alloc([128, 4, 512], BF16, f"ktst{i}") for i in range(2)]
    vst = [sb.alloc([128, 4, 4, 128], BF16, f"vst{i}") for i in range(2)]
    xTa_v = xT_all.rearrange("(k p) n -> p k n", p=128)
    kt_v = kt_s.rearrange("h p n -> p h n")
    v_v = v_s.rearrange("h t n -> t h n")
    NG = 32
    for gi in range(2):
        dma("pool", xTa[gi][:, :, :], xTa_v[:, :, gi * 512:(gi + 1) * 512], ("xTa", gi), writes=[("xTa", gi)])
    for gi in range(NG):
        sl = gi % 3
        if gi + 2 < NG:
            s2 = (gi + 2) % 3
            dma("pool", xTa[s2][:, :, :], xTa_v[:, :, (gi + 2) * 512:(gi + 3) * 512], ("xTa", s2),
                writes=[("xTa", s2)])
        xr = ("xTa", sl)
        st_i = gi % 2
        for h in range(4):
            k = next_bank()
            for kc in range(8):
                mm(bank(k), w_in_t[:, kc, 2048 + h * 128:2048 + (h + 1) * 128], xTa[sl][:, kc, :],
                   kc == 0, kc == 7, ["w_in", xr], [bankres(k)])
            cp("act" if h % 2 == 0 else "dve", ktst[st_i][:, h, :], bank(k), [bankres(k)], [("ktst", st_i)])
        dma("sp", kt_v[:, :, gi * 512:(gi + 1) * 512], ktst[st_i][:, :, :], ("ktst", st_i), reads=[("ktst", st_i)])
        for tl in range(4):
            k = next_bank()
            for kc in range(8):
                mm(bank(k), xTa[sl][:, kc, tl * 128:(tl + 1) * 128], w_in_t[:, kc, 2560:3072],
                   kc == 0, kc == 7, ["w_in", xr], [bankres(k)])
            cp("act" if tl % 2 == 1 else "dve", vst[st_i][:, :, tl, :],
               bank(k).rearrange("p (h d) -> p h d", d=128), [bankres(k)], [("vst", st_i)])
        dma("sp", v_v[:, :, gi * 512:(gi + 1) * 512],
            vst[st_i][:, :, :, :].rearrange("p h t d -> p h (t d)"), ("vst", st_i), reads=[("vst", st_i)])

    sb.reset(m_global)
    S.barrier()

    if PHASE_LIMIT < 4:
        return emit()
    if PHASE_LIMIT < 4:
        return emit()
    off_kt0 = sb.cur
    KT, Vb = [], []
    for i in range(2):
        KT.append(sb.alloc([128, SEQ], BF16, f"KT{i}"))
        Vb.append(sb.alloc([128, NKT, 129], BF16, f"Vb{i}"))
    w_ff1_t = nc.alloc_sbuf_tensor_at("w_ff1_t", [128, 8, 4096], BF16, offset=off_kt0)
    w1v = w_ff1.rearrange("(k p) n -> p k n", p=128)
    Pb = [sb.alloc([128, 2, 512], BF16, f"Pb{i}") for i in range(3)]
    epw = [sb.alloc([128, 384], F32, f"epw{i}") for i in range(2)]
    attn_t = [sb.alloc([128, 128], BF16, f"attn{i}") for i in range(2)]
    Ocp = sb.alloc([128, 8, 129], F32, "Ocp")
    a4 = sb.alloc([128, 4, 128], F32, "a4")
    for i in range(2):
        memset("dve", Vb[i][:, :, 128:129], 1.0, [("Vb1", i)])

    def load_head(h):
        hb = h % 2
        for q in range(4):
            dma("sp", KT[hb][:, q * 4096:(q + 1) * 4096], kt_s[h, :, q * 4096:(q + 1) * 4096],
                ("KT", hb, q), writes=[("KT", hb, q)])
            dma("sp", Vb[hb][:, 32 * q:32 * (q + 1), 0:128],
                v_s[h, :, q * 4096:(q + 1) * 4096].rearrange("p (k d) -> p k d", d=128),
                ("Vb", hb, q), writes=[("Vb", hb, q)])

    def Oacc(idx):
        return psB[idx // 3][:, (idx % 3) * 129:(idx % 3) * 129 + 129]

    deferred = []
    ep_cnt = [0]
    unit_cnt = [0]

    def flush_deferred(u=None):
        keep = []
        for (tu, f) in deferred:
            if u is None or tu == u:
                f()
            else:
                keep.append((tu, f))
        deferred[:] = keep

    load_head(0)
    for h in range(4):
        hb = h % 2
        if h + 1 < 4:
            load_head(h + 1)
        if h == 3:
            for kc2 in range(4):
                dma("pool", w_ff1_t[:, 2 * kc2:2 * kc2 + 2, :], w1v[:, 2 * kc2:2 * kc2 + 2, :], ("w_ff1", kc2),
                    writes=[("w_ff1", kc2), ("Vb1", 0)] + [("KT", 0, q) for q in range(4)]
                    + [("Vb", 0, q) for q in range(4)])
        for G in range(4):
            units = [(jp, rp) for jp in range(4 * G + 4) for rp in range(8)]

            def emit_S(u, jp, rp):
                kidx = jp * 8 + rp
                q = jp // 4
                i0 = max(0, jp - 4 * G)
                c0 = i0 * 128
                sbf = u % 2
                for c in range(2):
                    mm(psA[sbf][c * 64:(c + 1) * 64, c, c0:512] if False else psA[sbf][:, c, c0:512],
                       KT[hb][c * 64:(c + 1) * 64, kidx * 128:(kidx + 1) * 128],
                       QT[c * 64:(c + 1) * 64, h, G * 512 + c0:(G + 1) * 512],
                       True, True, [("KT", hb, q), ("QT", h, G)], [("ps", 2 * sbf + c)])

            def emit_exp(u, jp, rp):
                i0 = max(0, jp - 4 * G)
                c0 = i0 * 128
                sbf, pbf = u % 2, u % 3
                Sx, Px = psA[sbf], Pb[pbf]
                rS = [("ps", 2 * sbf), ("ps", 2 * sbf + 1)]
                if jp < 4 * G:
                    act(AF.Exp, Px[:, :, 0:512], Sx[:, :, 0:512], rS, [("P", pbf, 0)])
                else:
                    if rp == 0:
                        act(AF.Exp, Px[:, :, c0:c0 + 64], Sx[:, :, c0:c0 + 64], rS + ["dmask"],
                            [("P", pbf, 0)], bias=sc(DMASK))
                        act(AF.Exp, Px[:, :, c0 + 64:c0 + 128], Sx[:, :, c0 + 64:c0 + 128], rS,
                            [("P", pbf, 1)])
                    else:
                        act(AF.Exp, Px[:, :, c0:c0 + 128], Sx[:, :, c0:c0 + 128], rS + ["kmask"],
                            [("P", pbf, 0)], bias=sc(KM0 + rp))
                    if i0 < 3:
                        act(AF.Exp, Px[:, :, c0 + 128:512], Sx[:, :, c0 + 128:512], rS, [("P", pbf, 2)])

            def emit_AV(u, jp, rp):
                kidx = jp * 8 + rp
                q = jp // 4
                i0 = max(0, jp - 4 * G)
                pbf = u % 3
                for i in range(i0, 4):
                    for c in range(2):
                        idx = i * 2 + c
                        first = kidx == 0
                        last = kidx == 8 * (4 * G + i) + 7
                        mm(Oacc(idx), Pb[pbf][:, c, i * 128:(i + 1) * 128], Vb[hb][:, kidx, :],
                           first and (idx % 3 == 0), last,
                           [("P", pbf, 0), ("P", pbf, 1), ("P", pbf, 2), ("Vb", hb, q), ("Vb1", hb)],
                           [("ps", 4 + idx // 3)], skip=True)

            nU = len(units)
            emit_S(0, *units[0])
            emit_S(1, *units[1])
            for u in range(nU):
                emit_exp(u, *units[u])
                if u + 2 < nU:
                    emit_S(u + 2, *units[u + 2])
                emit_AV(u, *units[u])
                if deferred:
                    flush_deferred(u)
            for bk in range(3):
                n = 3 if bk < 2 else 2
                cp("dve", Ocp[:, 3 * bk:3 * bk + n, :],
                   psB[bk][:, 0:n * 129].rearrange("p (a d) -> p a d", d=129), [("ps", 4 + bk)], [("Ocp", bk)])

            def stage1(h=h, G=G):
                for i in range(4):
                    O0, O1 = Ocp[:, 2 * i, :], Ocp[:, 2 * i + 1, :]
                    ores = sorted({("Ocp", (2 * i) // 3), ("Ocp", (2 * i + 1) // 3)})
                    rl = small[:, 80 + 2 * i:82 + 2 * i]
                    a1 = epw[0][:, 0:128]
                    sq = epw[0][:, 128:256]
                    a = a4[:, i, :]
                    S.add("dve", lambda e, rl=rl, O0=O0: e.reciprocal(rl[:, 0:1], O0[:, 128:129]), reads=ores, writes=[("rl", i)])
                    S.add("dve", lambda e, rl=rl, O1=O1: e.reciprocal(rl[:, 1:2], O1[:, 128:129]), reads=ores + [("rl", i)], writes=[("rl", i)])
                    ts("dve", a1, O1[:, 0:128], rl[:, 1:2], sc(LAM), ALU.mult, ALU.mult, ores + [("rl", i), "lam"], ["epw_a1"])
                    stt(a, O0[:, 0:128], rl[:, 0:1], a1, ALU.mult, ALU.subtract, ores + [("rl", i), "epw_a1"], [("a4", i)])
                    tt("dve", sq, a, a, ALU.mult, [("a4", i)], ["epw_sq"])
                    S.add("dve", lambda e, i=i, sq=sq: e.reduce_sum(small[:, 96 + i:97 + i], sq, AX.X), reads=["epw_sq"], writes=[("ssq", i)])

            def stage2():
                act(AF.Ln, small[:, 100:104], small[:, 96:100], [("ssq", i) for i in range(4)] + ["eps"], ["ept"],
                    bias=sc(EPSC), scale=1.0 / 128.0)
                act(AF.Exp, small[:, 104:108], small[:, 100:104], ["ept"], ["eprstd"], scale=-0.5)

            def stage3(h=h, G=G):
                for i in range(4):
                    e_i = ep_cnt[0] % 2
                    ep_cnt[0] += 1
                    stt(attn_t[e_i][:, :], a4[:, i, :], small[:, 104 + i:105 + i], g08[:, :], ALU.mult, ALU.mult,
                        [("a4", i), "eprstd", "g08"], [("attn", e_i)])
                    S.add("pe", lambda e, i=i, e_i=e_i: e.transpose(psT[:, i * 128:(i + 1) * 128], attn_t[e_i][:, :], ident_b[:, :]),
                          reads=[("attn", e_i), "identb"], writes=[("ps", 7)])
                    cp("dve", mixedT[:, 4 + h, (4 * G + i) * 128:(4 * G + i + 1) * 128],
                       psT[:, i * 128:(i + 1) * 128], [("ps", 7)], [("mixedT", 4 + h, G, i)])

            deferred.append((1, stage1))
            deferred.append((12, stage2))
            deferred.append((16, stage3))
    flush_deferred()

    w_out_t = nc.alloc_sbuf_tensor_at("w_out_t", [128, 8, D], BF16, offset=m_g2)
    dma("pool", w_out_t[:, :, :], w_out.rearrange("(k p) n -> p k n", p=128), "w_out",
        writes=["w_out"] + [("QT", h, G) for h in range(4) for G in range(4)])

    sb.reset(m_g2)
    S.barrier()

    if PHASE_LIMIT < 5:
        return emit()
    sb.cur = m_g2 + 128 * 8 * 2 * 8
    assert sb.cur <= off_kt0, (sb.cur, off_kt0)
    sb.cur = off_kt0 + 65536
    w2a = sb.alloc_top([128, 16, D], BF16, "w2a")
    w2v = w_ff2.rearrange("(k p) n -> p k n", p=128)
    dma("pool", w2a[:, :, :], w2v[:, 0:16, :], "w2a", writes=["w2a"])
    xt = [sb.alloc([128, D], F32, f"xt{i}") for i in range(2)]
    rt = [sb.alloc([128, D], F32, f"rt{i}") for i in range(2)]
    tmpt0 = sb.alloc([128, D], F32, "tmpt")
    tmpt = [tmpt0, tmpt0]
    ln1_t = sb.alloc([128, 2, D], F32, "ln1")
    dma("sp", ln1_t[:, :, :], lnp[:, 0:2, :], "ln1", writes=["ln1"])
    x1t = [sb.alloc([128, D], F32, f"x1t{i}") for i in range(2)]
    x1Tst = [sb.alloc([128, 8, 128], BF16, f"x1Tst{i}") for i in range(2)]
    x1Tv = x1T_s.rearrange("(k p) n -> p k n", p=128)
    assert sb.cur <= sb.top_cur, (sb.cur, sb.top_cur)

    def c_outproj(tile):
        P = 128 if tile < 16 else 64
        r0 = tile * 128
        sl = tile % 2
        src = x_own[r0:r0 + 128, :] if tile < 16 else xs_d
        dma("sp", xt[sl][0:P, :], src, ("xt", sl), writes=[("xt", sl)])
        pa = psA[sl]
        for half in range(2):
            for mc in range(8):
                lhs = mixedT[:, mc, r0:r0 + 128] if tile < 16 else mixedT_s[:, mc, :]
                rd = ["w_out"]
                if tile < 16:
                    if mc < 4:
                        rd.append(("mixedT", mc, tile // 4))
                    else:
                        rd.append(("mixedT", mc, tile // 4, tile % 4))
                else:
                    rd.append("mixedTs")
                mm(pa[0:P, half, :], lhs, w_out_t[:, mc, half * 512:(half + 1) * 512], mc == 0, mc == 7,
                   rd, [("ps", 2 * sl + half)])

    def c_head(tile):
        P = 128 if tile < 16 else 64
        sl = tile % 2
        pa = psA[sl]
        stt(rt[sl][0:P, :], xt[sl][0:P, :], ALPHA, pa[0:P, :, :].rearrange("p a n -> p (a n)"), ALU.mult, ALU.add,
            [("xt", sl), ("ps", 2 * sl), ("ps", 2 * sl + 1)], [("rt", sl)])
        ln_head(P, rt[sl][0:P, :], ("rt", sl), 40 + 20 * sl)

    def c_post(tile):
        P = 128 if tile < 16 else 64
        r0 = tile * 128
        sl = tile % 2
        pa = psA[sl]
        ln_tail(P, rt[sl][0:P, :], ("rt", sl), x1t[sl][0:P, :], ("x1t", sl), (ln1_t, "ln1"), tmpt[sl][0:P, :],
                ("tmpt", 0), 40 + 20 * sl, part=2)
        dma("pool", x1_s[r0:r0 + P, :], x1t[sl][0:P, :], ("x1t", sl), reads=[("x1t", sl)], writes=[("x1s", tile)])
        for kc in range(8):
            pb_ = psB[kc // 4]
            S.add("pe", lambda e, kc=kc, pb_=pb_, P=P, sl=sl: e.transpose(
                pb_[:, (kc % 4) * 128:(kc % 4) * 128 + P], x1t[sl][0:P, kc * 128:(kc + 1) * 128], ident_f[0:P, 0:P]),
                reads=[("x1t", sl), "identf"], writes=[("ps", 4 + kc // 4)])
        for hf in range(2):
            cp("act", x1Tst[sl][:, 4 * hf:4 * hf + 4, 0:P],
               psB[hf][:, :].rearrange("p (k n) -> p k n", n=128)[:, :, 0:P], [("ps", 4 + hf)], [("x1Tst", sl)])
        dma("pool", x1Tv[:, :, r0:r0 + P], x1Tst[sl][:, :, 0:P], ("x1Tst", sl), reads=[("x1Tst", sl)],
            writes=[("x1Ts", tile)])

    def c_norm(tile):
        P = 128 if tile < 16 else 64
        sl = tile % 2
        ln_tail(P, rt[sl][0:P, :], ("rt", sl), x1t[sl][0:P, :], ("x1t", sl), (ln1_t, "ln1"), tmpt[sl][0:P, :],
                ("tmpt", 0), 40 + 20 * sl, part=1)

    c_outproj(0)
    c_outproj(1)
    c_head(0)
    for tile in range(17):
        c_norm(tile)
        if tile + 1 < 17:
            c_head(tile + 1)
        if tile + 2 < 17:
            c_outproj(tile + 2)
        c_post(tile)

    sb.reset(m_consts)
    S.barrier()

    if PHASE_LIMIT < 6:
        return emit()
    w2b = sb.alloc([128, 16, D], BF16, "w2b")
    dma("pool", w2b[:, :, :], w2v[:, 16:32, :], "w2b", writes=["w2b"])
    x1Tg0 = sb.alloc([128, 8, 512], BF16, "x1Tg0")
    ln2_t = sb.alloc([128, 2, D], F32, "ln2")
    dma("sp", ln2_t[:, :, :], lnp[:, 2:4, :], "ln2", writes=["ln2"])
    assert sb.cur <= off_kt0, (sb.cur, off_kt0)
    sb.cur = off_kt0 + 65536
    x1Tg = [x1Tg0, sb.alloc([128, 8, 512], BF16, "x1Tg1")]
    hT = sb.alloc([128, 32, 512], BF16, "hT")
    x1g = sb.alloc([128, D], F32, "x1g")
    rt2 = sb.alloc([128, D], F32, "rt2")
    tmp2 = sb.alloc([128, D], F32, "tmp2")
    yst = sb.alloc([128, D], F32, "yst")
    relu_t = sb.alloc([128, 512], F32, "relu")
    assert sb.cur <= sb.top_cur, (sb.cur, sb.top_cur)

    NGRP = 5

    def d_load(g2):
        N2 = 512 if g2 < 4 else 64
        t2 = [4 * g2 + k_ for k_ in range(4)] if g2 < 4 else [16]
        dma("sp", x1Tg[g2 % 2][:, :, 0:N2], x1Tv[:, :, g2 * 512:g2 * 512 + N2], ("x1Tg", g2 % 2),
            reads=[("x1Ts", t) for t in t2], writes=[("x1Tg", g2 % 2)])

    for grp in range(NGRP):
        N = 512 if grp < 4 else 64
        sl = grp % 2
        tiles = [4 * grp + k_ for k_ in range(4)] if grp < 4 else [16]
        for g2 in ([0, 1] if grp == 0 else [grp + 1]):
            if g2 < NGRP:
                d_load(g2)
        for fc in range(32):
            kb = fc % 3
            for kc in range(8):
                mm(psB[kb][:, 0:N], w_ff1_t[:, kc, fc * 128:(fc + 1) * 128], x1Tg[sl][:, kc, 0:N],
                   kc == 0, kc == 7, [("w_ff1", kc // 2), ("x1Tg", sl)], [("ps", 4 + kb)])
            act(AF.Relu, relu_t[:, 0:N], psB[kb][:, 0:N], [("ps", 4 + kb)], ["relu"])
            tt("dve", hT[:, fc, 0:N], relu_t[:, 0:N], relu_t[:, 0:N], ALU.mult, ["relu"], [("hT", fc)])
        for ti, tile in enumerate(tiles):
            P = 128 if tile < 16 else 64
            r0 = tile * 128
            s2 = tile % 2
            dma("sp", x1g[0:P, :], x1_s[r0:r0 + P, :], "x1g", reads=[("x1s", tile)], writes=["x1g"])
            pa = psA[s2]
            for half in range(2):
                for fc in range(32):
                    w2 = w2a if fc < 16 else w2b
                    mm(pa[0:P, half, :], hT[:, fc, ti * 128:ti * 128 + P],
                       w2[:, fc % 16, half * 512:(half + 1) * 512], fc == 0, fc == 31,
                       [("hT", fc), "w2a" if fc < 16 else "w2b"], [("ps", 2 * s2 + half)])
            stt(rt2[0:P, :], x1g[0:P, :], ALPHA, pa[0:P, :, :].rearrange("p a n -> p (a n)"),
                ALU.mult, ALU.add, ["x1g", ("ps", 2 * s2), ("ps", 2 * s2 + 1)], ["rt2"])
            layer_norm(P, rt2[0:P, :], "rt2", yst[0:P, :], "yst", (ln2_t, "ln2"), tmp2[0:P, :], "tmp2", 40)
            dst = y_own[r0:r0 + 128, :] if tile < 16 else ys_o
            dma("pool", dst, yst[0:P, :], "yst", reads=["yst"])

    return emit()


_NC_CACHE = {}


def _get_nc():
    if "nc" not in _NC_CACHE:
        _NC_CACHE["nc"] = build_program()
    return _NC_CACHE["nc"]


def kernel(x_prompt, x_sample, cache_k, cache_v, state_conv, w_in, conv_w,
           lambda_q1, lambda_k1, lambda_q2, lambda_k2, subln_g, w_out,
           ln1_g, ln1_b, w_ff1, w_ff2, ln2_g, ln2_b):
    f32 = np.float32
    X = np.asarray(x_prompt, f32)[0]
    Xt = X.reshape(128, 128, D)
    xs_all = np.asarray(x_sample, f32)
    ck = np.asarray(cache_k, f32)[0]
    cv = np.asarray(cache_v, f32)[0]
    stc = np.asarray(state_conv, f32)[0]
    lnp = np.broadcast_to(np.stack([ln1_g[0], ln1_b[0], ln2_g[0], ln2_b[0]]).astype(f32)[None], (128, 4, D))
    subg = np.broadcast_to(np.asarray(subln_g, f32)[0][None, :], (128, 128))
    lamv = np.broadcast_to(np.stack([lambda_q1[0], lambda_k1[0], lambda_q2[0], lambda_k2[0]]).astype(f32)[None],
                           (128, 4, 64))
    convw = np.asarray(conv_w, f32)[0].T.reshape(4, 128, 3).transpose(1, 0, 2)
    shared = {
        "w_in": np.ascontiguousarray(w_in[0], f32), "w_out": np.ascontiguousarray(w_out[0], f32),
        "w_ff1": np.ascontiguousarray(w_ff1[0], f32), "w_ff2": np.ascontiguousarray(w_ff2[0], f32),
        "lnp": np.ascontiguousarray(lnp), "subg": np.ascontiguousarray(subg),
        "lamv": np.ascontiguousarray(lamv), "convw": np.ascontiguousarray(convw),
        "ident": np.eye(128, dtype=f32),
    }
    in_maps = []
    for c in range(NCORES):
        order = [8 * j + (r + c) % 8 for j in range(16) for r in range(8)]
        own = [8 * j + c for j in range(16)]
        x_own = Xt[own].reshape(2048, D)
        halo = np.zeros((16, 2, D), f32)
        for j in range(16):
            s0 = own[j] * 128
            if s0 >= 2:
                halo[j] = X[s0 - 2:s0]
        xs = xs_all[4 * c:4 * c + 4].reshape(64, D)
        kcT = ck[4 * c:4 * c + 4].transpose(0, 2, 3, 1).reshape(16, 128, 2048)
        vcl = cv[4 * c:4 * c + 4].reshape(4, 16, 128, 4, 128).transpose(0, 3, 2, 1, 4).reshape(16, 128, 16, 128)
        stT = stc[4 * c:4 * c + 4].reshape(4, 2, 4, 128).transpose(3, 2, 0, 1)
        m = dict(shared)
        m.update({
            "xT_all": np.ascontiguousarray(Xt[order].reshape(SEQ, D).T) if PHASE_LIMIT >= 3 else np.ascontiguousarray(X[0:512].T),
            "xT_own": np.ascontiguousarray(x_own.T),
            "x_own": np.ascontiguousarray(x_own),
            "xT_halo": np.ascontiguousarray(halo.reshape(32, D).T),
            "cid": np.full((128, 1), float(c), f32),
            "xsT": np.ascontiguousarray(xs.T), "xs": np.ascontiguousarray(xs),
            "kcT": np.ascontiguousarray(kcT if PHASE_LIMIT >= 1 else kcT[0:1]),
            "vc": np.ascontiguousarray(vcl if PHASE_LIMIT >= 1 else vcl[0:1]),
            "stT": np.ascontiguousarray(stT),
        })
        in_maps.append(m)
    nc = _get_nc()
    res = run_bass_kernel_spmd(nc, in_maps, core_ids=list(range(NCORES)))
    R = res.results
    y_prompt = np.zeros((128, 128, D), f32)
    k_prompt = np.zeros((128, 128, 512), f32)
    v_prompt = np.zeros((128, 128, 512), f32)
    y_sample = np.zeros((32, 16, D), f32)
    k_sample = np.zeros((32, 16, 512), f32)
    v_sample = np.zeros((32, 16, 512), f32)
    conv_sample = np.zeros((32, 2, 512), f32)
    for c in range(NCORES):
        own = [8 * j + c for j in range(16)]
        y_prompt[own] = R[c]["y_own"].reshape(16, 128, D)
        k_prompt[own] = R[c]["k_own"].reshape(16, 128, 512)
        v_prompt[own] = R[c]["v_own"].reshape(16, 128, 512)
        y_sample[4 * c:4 * c + 4] = R[c]["ys"].reshape(4, 16, D)
        k_sample[4 * c:4 * c + 4] = R[c]["ks"].reshape(4, 16, 512)
        v_sample[4 * c:4 * c + 4] = R[c]["vs"].reshape(4, 16, 512)
        conv_sample[4 * c:4 * c + 4] = R[c]["conv_s"].transpose(2, 3, 1, 0).reshape(4, 2, 512)
    conv_prompt = R[7]["conv_p"].transpose(2, 1, 0).reshape(1, 1, 2, 512)
    return (y_prompt.reshape(1, SEQ, D), y_sample,
            k_prompt.reshape(1, 1, SEQ, 4, 128), v_prompt.reshape(1, 1, SEQ, 4, 128),
            np.ascontiguousarray(conv_prompt),
            k_sample.reshape(1, 32, 16, 4, 128), v_sample.reshape(1, 32, 16, 4, 128),
            conv_sample.reshape(1, 32, 2, 512))
```
